# Optimizing a Trainium2 kernel written in Bass

```python
import math
import jax, jax.numpy as jnp
from jax import lax
import numpy as np

D_MODEL = 2048
BATCH = 16
SEQ = 256
DEPTH = 2
DEC_BATCH = 4
DEC_SEQ = 1024
PAST_LEN = 512

GRID_W = 64
N_EVEN = (DEPTH + 1) // 2
N_ODD = DEPTH // 2
EPS = 1e-6
ATT_HEADS = 16
ATT_KV_HEADS = 4
ATT_GROUPS = ATT_HEADS // ATT_KV_HEADS
HEAD_DIM = 64
WINDOW = 128
BLOCK = 128
ROPE_BASE = 10000.0
ATT_WIDTH = ATT_HEADS * HEAD_DIM
KV_WIDTH = ATT_KV_HEADS * HEAD_DIM
SGU_GROUPS = 8
SGU_GROUP_DIM = 128
SGU_CHUNK = 128
SGU_WIDTH = SGU_GROUPS * SGU_GROUP_DIM
EVEN_IN = ATT_WIDTH + 2 * KV_WIDTH + 2 * SGU_WIDTH
EVEN_OUT = ATT_WIDTH + SGU_WIDTH
EVEN_SPLITS = (ATT_WIDTH, ATT_WIDTH + KV_WIDTH, ATT_WIDTH + 2 * KV_WIDTH,
               ATT_WIDTH + 2 * KV_WIDTH + SGU_WIDTH)
RET_HEADS = 8
RET_DK = D_MODEL // RET_HEADS
RET_DV = 2 * RET_DK
RET_CHUNK = 128
RET_QK = RET_HEADS * RET_DK
RET_V = RET_HEADS * RET_DV
ODD_IN = 2 * RET_QK + 2 * RET_V
ODD_SPLITS = (RET_QK, 2 * RET_QK, 2 * RET_QK + RET_V)
FF_MULTIPLE = 256
D_FF = -(-8 * D_MODEL // (3 * FF_MULTIPLE)) * FF_MULTIPLE

kernel_name = 'hybrid_window_sgu_retention_diffusion_step'

F32 = jnp.float32


def rmsnorm(x, g):
    xf = x.astype(F32)
    y = xf * lax.rsqrt(jnp.mean(xf * xf, axis=-1, keepdims=True) + EPS)
    return (y * g.astype(F32)).astype(x.dtype)


def adaln(cvec, w, b):
    m = jax.nn.silu(cvec) @ w + b
    return jnp.split(m, 6, axis=-1)


def modulate(h, shift, scale):
    return h * (1 + scale[..., None, :]) + shift[..., None, :]


def axial_rope(L):
    rows = L // GRID_W
    t_row = jnp.repeat(jnp.arange(rows), GRID_W).astype(F32)
    t_col = jnp.tile(jnp.arange(GRID_W), rows).astype(F32)
    n_freq = HEAD_DIM // 4
    inv = ROPE_BASE ** (-jnp.arange(n_freq, dtype=F32) / n_freq)
    ang = jnp.stack([t_row[:, None] * inv, t_col[:, None] * inv], axis=1)
    return jnp.cos(ang), jnp.sin(ang)


def apply_rope(x, cos, sin):
    shp = x.shape
    xr = x.astype(F32).reshape(shp[:-1] + (2, 2, HEAD_DIM // 4))
    x1, x2 = xr[..., 0, :], xr[..., 1, :]
    bshape = (1, shp[1]) + (1,) * (len(shp) - 3) + (2, HEAD_DIM // 4)
    cb, sb = cos.reshape(bshape), sin.reshape(bshape)
    out = jnp.stack([x1 * cb - x2 * sb, x2 * cb + x1 * sb], axis=-2)
    return out.reshape(shp).astype(x.dtype)


def even_project(h, w_in):
    B, L, _ = h.shape
    z = h @ w_in
    q, k, v, u, vg = jnp.split(z, EVEN_SPLITS, axis=-1)
    q = q.reshape(B, L, ATT_KV_HEADS, ATT_GROUPS, HEAD_DIM)
    k = k.reshape(B, L, ATT_KV_HEADS, HEAD_DIM)
    v = v.reshape(B, L, ATT_KV_HEADS, HEAD_DIM)
    return q, k, v, jax.nn.gelu(u), jax.nn.gelu(vg)


def spatial_gating(u, vg, ln_g, ln_b, w_s, b_s):
    B, L, _ = u.shape
    nc = L // SGU_CHUNK
    vf = vg.astype(F32)
    mu = jnp.mean(vf, axis=-1, keepdims=True)
    var = jnp.mean(jnp.square(vf - mu), axis=-1, keepdims=True)
    vn = ((vf - mu) * lax.rsqrt(var + EPS) * ln_g.astype(F32) + ln_b.astype(F32)).astype(u.dtype)
    vc = vn.reshape(B, nc, SGU_CHUNK, SGU_GROUPS, SGU_GROUP_DIM)
    mixed = jnp.einsum('gpq,bnqgc->bnpgc', w_s, vc) + b_s.T[None, None, :, :, None]
    return u * mixed.reshape(B, L, SGU_WIDTH)


def attn_context(q, k, v, sink):
    B, L = q.shape[:2]
    s = jnp.einsum('blkgd,bmkd->bkglm', q, k).astype(F32) * (HEAD_DIM ** -0.5)
    sk = jnp.broadcast_to(sink.astype(F32)[None, :, :, None, None], s.shape[:-1] + (1,))
    p = jax.nn.softmax(jnp.concatenate([sk, s], axis=-1), axis=-1)[..., 1:]
    o = jnp.einsum('bkglm,bmkd->blkgd', p.astype(v.dtype), v)
    return o.reshape(B, L, ATT_WIDTH)


def attn_latent(q, k, v, k_ctx, v_ctx, sink):
    B, L = q.shape[:2]
    nb = L // BLOCK
    qb = q.reshape(B, nb, BLOCK, ATT_KV_HEADS, ATT_GROUPS, HEAD_DIM)
    pad = ((0, 0), (BLOCK, BLOCK), (0, 0), (0, 0))
    kp = jnp.pad(k, pad).reshape(B, nb + 2, BLOCK, ATT_KV_HEADS, HEAD_DIM)
    vp = jnp.pad(v, pad).reshape(B, nb + 2, BLOCK, ATT_KV_HEADS, HEAD_DIM)

    def band(xp):
        return jnp.concatenate([xp[:, :-2], xp[:, 1:-1], xp[:, 2:]], axis=2)

    kb, vb = band(kp), band(vp)
    scale = HEAD_DIM ** -0.5
    s_loc = jnp.einsum('bnqkgd,bnskd->bkgnqs', qb, kb).astype(F32) * scale
    qpos = jnp.arange(nb)[:, None, None] * BLOCK + jnp.arange(BLOCK)[None, :, None]
    kpos = (jnp.arange(nb)[:, None, None] - 1) * BLOCK + jnp.arange(3 * BLOCK)[None, None, :]
    valid = (jnp.abs(qpos - kpos) <= WINDOW) & (kpos >= 0) & (kpos < L)
    s_loc = jnp.where(valid, s_loc, -jnp.inf)
    s_ctx = jnp.einsum('bnqkgd,bmkd->bkgnqm', qb, k_ctx).astype(F32) * scale
    sk = jnp.broadcast_to(sink.astype(F32)[None, :, :, None, None, None], s_loc.shape[:-1] + (1,))
    p = jax.nn.softmax(jnp.concatenate([sk, s_loc, s_ctx], axis=-1), axis=-1)
    p_loc = p[..., 1:1 + 3 * BLOCK].astype(v.dtype)
    p_ctx = p[..., 1 + 3 * BLOCK:].astype(v.dtype)
    o = (jnp.einsum('bkgnqs,bnskd->bnqkgd', p_loc, vb)
         + jnp.einsum('bkgnqm,bmkd->bnqkgd', p_ctx, v_ctx))
    return o.reshape(B, L, ATT_WIDTH)


def even_mixer_context(h, w_in, w_out, sink, ln_g, ln_b, w_s, b_s):
    q, k, v, u, vg = even_project(h, w_in)
    sink_kg = sink.reshape(ATT_KV_HEADS, ATT_GROUPS)
    a = attn_context(q, k, v, sink_kg)
    g = spatial_gating(u, vg, ln_g, ln_b, w_s, b_s)
    return jnp.concatenate([a, g], axis=-1) @ w_out, k, v


def even_mixer_latent(h, k_ctx, v_ctx, cos, sin, w_in, w_out, sink, ln_g, ln_b, w_s, b_s):
    q, k, v, u, vg = even_project(h, w_in)
    q = apply_rope(q, cos, sin)
    k = apply_rope(k, cos, sin)
    sink_kg = sink.reshape(ATT_KV_HEADS, ATT_GROUPS)
    a = attn_latent(q, k, v, k_ctx, v_ctx, sink_kg)
    g = spatial_gating(u, vg, ln_g, ln_b, w_s, b_s)
    return jnp.concatenate([a, g], axis=-1) @ w_out


def retention_scan(q, k, v, log_g, s0):
    B, H, L, _ = q.shape
    C = RET_CHUNK
    nc = L // C
    idx = jnp.arange(C, dtype=F32)
    rel = idx[:, None] - idx[None, :]
    dmask = jnp.exp(jnp.where(rel >= 0, log_g[:, None, None] * rel, -jnp.inf))
    q_in = jnp.exp(log_g[:, None] * (idx + 1))[None, :, :, None]
    k_out = jnp.exp(log_g[:, None] * (C - 1 - idx))[None, :, :, None]
    g_chunk = jnp.exp(log_g * C)[None, :, None, None]

    def chunks(a):
        return jnp.moveaxis(a.reshape(B, H, nc, C, a.shape[-1]), 2, 0)

    def step(S, xs):
        qc, kc, vc = xs
        scores = jnp.einsum('bhid,bhjd->bhij', qc, kc) * dmask
        intra = jnp.einsum('bhij,bhje->bhie', scores, vc)
        inter = jnp.einsum('bhid,bhde->bhie', qc, S) * q_in
        S = S * g_chunk + jnp.einsum('bhjd,bhje->bhde', kc * k_out, vc)
        return S, intra + inter

    S, o = lax.scan(step, s0, (chunks(q), chunks(k), chunks(v)))
    o = jnp.moveaxis(o, 0, 2).reshape(B, H, L, v.shape[-1])
    return o, S


def retention_mixer(h, w_in, w_out, gn_g, dec_f, dec_b, s0_f, s0_b):
    B, L, _ = h.shape
    z = h @ w_in
    q, k, v, g = jnp.split(z, ODD_SPLITS, axis=-1)

    def heads(a, d):
        return a.astype(F32).reshape(B, L, RET_HEADS, d).transpose(0, 2, 1, 3)

    qh = heads(q, RET_DK)
    kh = heads(k, RET_DK) * (RET_DK ** -0.5)
    vh = heads(v, RET_DV)
    lg_f = jax.nn.log_sigmoid(dec_f.astype(F32))
    lg_b = jax.nn.log_sigmoid(dec_b.astype(F32))
    o_f, s_f = retention_scan(qh, kh, vh, lg_f, s0_f.astype(F32))
    flip = lambda a: jnp.flip(a, axis=2)
    o_b, s_b = retention_scan(flip(qh), flip(kh), flip(vh), lg_b, s0_b.astype(F32))
    o = o_f + flip(o_b)
    mu = jnp.mean(o, axis=-1, keepdims=True)
    var = jnp.mean(jnp.square(o - mu), axis=-1, keepdims=True)
    o = ((o - mu) * lax.rsqrt(var + EPS)).transpose(0, 2, 1, 3).reshape(B, L, RET_V)
    o = o * gn_g.astype(F32)
    out = (jax.nn.silu(g.astype(F32)) * o).astype(h.dtype) @ w_out
    return out, s_f.astype(h.dtype), s_b.astype(h.dtype)


def swiglu(h, w_gate, w_up, w_down):
    return (jax.nn.silu(h @ w_gate) * (h @ w_up)) @ w_down


def setup_inputs(seed: int = 0) -> dict:
    key = jax.random.key(seed)
    ks = iter(jax.random.split(key, 40))

    def nrm(shape, scale):
        return jax.random.normal(next(ks), shape, jnp.float32) * scale

    D = D_MODEL
    base = 1.0 - 2.0 ** (-5.0 - np.arange(RET_HEADS))
    decay_logit = jnp.asarray(np.log(base / (1.0 - base)).astype(np.float32))
    return {
        'x_prompt': nrm((BATCH, SEQ, D), 1.0),
        'x_sample': nrm((DEC_BATCH, DEC_SEQ, D), 1.0),
        'cache_attn_k': nrm((DEC_BATCH, N_EVEN, PAST_LEN, ATT_KV_HEADS, HEAD_DIM), 1.0),
        'cache_attn_v': nrm((DEC_BATCH, N_EVEN, PAST_LEN, ATT_KV_HEADS, HEAD_DIM), 1.0),
        'state_ret_fwd': nrm((DEC_BATCH, N_ODD, RET_HEADS, RET_DK, RET_DV), 0.5),
        'state_ret_bwd': nrm((DEC_BATCH, N_ODD, RET_HEADS, RET_DK, RET_DV), 0.5),
        'c': nrm((DEC_BATCH, D), 1.0),
        'c_ctx': nrm((D,), 1.0),
        'ada_w': nrm((DEPTH, D, 6 * D), 0.5 * D ** -0.5),
        'ada_b': nrm((DEPTH, 6 * D), 0.01),
        'norm_mix_g': 1.0 + nrm((DEPTH, D), 0.02),
        'norm_ffn_g': 1.0 + nrm((DEPTH, D), 0.02),
        'even_w_in': nrm((N_EVEN, D, EVEN_IN), D ** -0.5),
        'even_w_out': nrm((N_EVEN, EVEN_OUT, D), EVEN_OUT ** -0.5),
        'attn_sink': nrm((N_EVEN, ATT_HEADS), 0.5),
        'sgu_ln_g': 1.0 + nrm((N_EVEN, SGU_WIDTH), 0.02),
        'sgu_ln_b': nrm((N_EVEN, SGU_WIDTH), 0.01),
        'sgu_w': nrm((N_EVEN, SGU_GROUPS, SGU_CHUNK, SGU_CHUNK), SGU_CHUNK ** -0.5),
        'sgu_b': 1.0 + nrm((N_EVEN, SGU_GROUPS, SGU_CHUNK), 0.01),
        'ret_w_in': nrm((N_ODD, D, ODD_IN), D ** -0.5),
        'ret_w_out': nrm((N_ODD, RET_V, D), RET_V ** -0.5),
        'ret_gn_g': 1.0 + nrm((N_ODD, RET_V), 0.02),
        'ret_decay_fwd': decay_logit[None] + nrm((N_ODD, RET_HEADS), 0.1),
        'ret_decay_bwd': decay_logit[None] + nrm((N_ODD, RET_HEADS), 0.1),
        'ffn_w_gate': nrm((DEPTH, D, D_FF), D ** -0.5),
        'ffn_w_up': nrm((DEPTH, D, D_FF), D ** -0.5),
        'ffn_w_down': nrm((DEPTH, D_FF, D), D_FF ** -0.5),
        'final_g': 1.0 + nrm((D,), 0.02),
    }


def reference(x_prompt, x_sample, cache_attn_k, cache_attn_v, state_ret_fwd, state_ret_bwd,
              c, c_ctx, ada_w, ada_b, norm_mix_g, norm_ffn_g, even_w_in, even_w_out,
              attn_sink, sgu_ln_g, sgu_ln_b, sgu_w, sgu_b, ret_w_in, ret_w_out, ret_gn_g,
              ret_decay_fwd, ret_decay_bwd, ffn_w_gate, ffn_w_up, ffn_w_down, final_g):
    bp = x_prompt.shape[0]
    cos, sin = axial_rope(x_sample.shape[1])
    xp, xs = x_prompt, x_sample
    new_k, new_v, new_sf, new_sb = [], [], [], []
    for i in range(DEPTH):
        j = i // 2
        sh_p, sc_p, ga_p, sh2_p, sc2_p, ga2_p = adaln(c_ctx, ada_w[i], ada_b[i])
        sh_s, sc_s, ga_s, sh2_s, sc2_s, ga2_s = adaln(c, ada_w[i], ada_b[i])
        hp = modulate(rmsnorm(xp, norm_mix_g[i]), sh_p, sc_p)
        hs = modulate(rmsnorm(xs, norm_mix_g[i]), sh_s, sc_s)
        if i % 2 == 0:
            op, kc, vc = even_mixer_context(hp, even_w_in[j], even_w_out[j], attn_sink[j],
                                            sgu_ln_g[j], sgu_ln_b[j], sgu_w[j], sgu_b[j])
            os_ = even_mixer_latent(hs, cache_attn_k[:, j], cache_attn_v[:, j], cos, sin,
                                    even_w_in[j], even_w_out[j], attn_sink[j],
                                    sgu_ln_g[j], sgu_ln_b[j], sgu_w[j], sgu_b[j])
            new_k.append(kc)
            new_v.append(vc)
        else:
            zeros = jnp.zeros((bp, RET_HEADS, RET_DK, RET_DV), F32)
            op, sf, sb = retention_mixer(hp, ret_w_in[j], ret_w_out[j], ret_gn_g[j],
                                         ret_decay_fwd[j], ret_decay_bwd[j], zeros, zeros)
            os_, _, _ = retention_mixer(hs, ret_w_in[j], ret_w_out[j], ret_gn_g[j],
                                        ret_decay_fwd[j], ret_decay_bwd[j],
                                        state_ret_fwd[:, j], state_ret_bwd[:, j])
            new_sf.append(sf)
            new_sb.append(sb)
        xp = xp + ga_p[..., None, :] * op
        xs = xs + ga_s[..., None, :] * os_
        hp = modulate(rmsnorm(xp, norm_ffn_g[i]), sh2_p, sc2_p)
        hs = modulate(rmsnorm(xs, norm_ffn_g[i]), sh2_s, sc2_s)
        xp = xp + ga2_p[..., None, :] * swiglu(hp, ffn_w_gate[i], ffn_w_up[i], ffn_w_down[i])
        xs = xs + ga2_s[..., None, :] * swiglu(hs, ffn_w_gate[i], ffn_w_up[i], ffn_w_down[i])
    y_prompt = rmsnorm(xp, final_g)
    y_sample = rmsnorm(xs, final_g)
    new_attn_k = jnp.stack(new_k, axis=1)
    new_attn_v = jnp.stack(new_v, axis=1)
    new_ret_fwd = jnp.stack(new_sf, axis=1)
    new_ret_bwd = jnp.stack(new_sb, axis=1)
    return (y_prompt, y_sample, new_attn_k, new_attn_v, new_ret_fwd, new_ret_bwd)
```

```python
import numpy as np
from contextlib import ExitStack
import concourse.bass as bass
import concourse.mybir as mybir
from concourse.bass_utils import run_bass_kernel_spmd

F32 = mybir.dt.float32
BF16 = mybir.dt.bfloat16
AF = mybir.ActivationFunctionType
ALU = mybir.AluOpType

T = 1024
D = 2048
KT = 16
DFF = 5632
EPS = 1e-6
NEG = -30000.0


class Sched:
    COMPUTE = ('pe', 'dve', 'act')
    DMAQ = ('sp', 'pool')

    def __init__(self, nc, es, n_dma_sems=8):
        self.nc = nc
        self.streams = {e: [] for e in self.COMPUTE + self.DMAQ}
        self.esem = {e: es.enter_context(nc.semaphore("sem_" + e)) for e in self.COMPUTE}
        self.ecnt = {e: 0 for e in self.COMPUTE}
        self.dsem = {q: [es.enter_context(nc.semaphore("dsem_%s_%d" % (q, i))) for i in range(n_dma_sems)]
                     for q in self.DMAQ}
        self.dcnt = {q: [0] * n_dma_sems for q in self.DMAQ}
        self.drr = {q: 0 for q in self.DMAQ}
        self.waited = {e: {} for e in self.streams}
        self.last_w = {}
        self.readers = {}

    def _wait(self, eng, sem, val):
        sid = id(sem)
        if self.waited[eng].get(sid, 0) >= val:
            return
        self.waited[eng][sid] = val
        self.streams[eng].append(('w', sem, val))

    def _deps(self, eng, reads, writes):
        toks = []
        for k in reads:
            w = self.last_w.get(k)
            if w is not None:
                toks.append(w)
            if isinstance(k, tuple) and k[0] in ('ps', 'pb'):
                toks.extend(t for t in self.readers.get(k, ()) if t[2] != eng)
        for k in writes:
            w = self.last_w.get(k)
            if w is not None:
                toks.append(w)
            toks.extend(self.readers.get(k, ()))
        for t in toks:
            if t[2] == eng and eng == 'pe':
                continue
            self._wait(eng, t[0], t[1])

    def _commit(self, tok, reads, writes):
        for k in writes:
            self.last_w[k] = tok
            self.readers[k] = []
        for k in reads:
            if k in writes:
                continue
            self.readers.setdefault(k, []).append(tok)

    def op(self, eng, fn, reads=(), writes=()):
        self._deps(eng, reads, writes)
        self.ecnt[eng] += 1
        tok = (self.esem[eng], self.ecnt[eng], eng)
        self.streams[eng].append(('o', fn, self.esem[eng], 1))
        self._commit(tok, reads, writes)

    def dma(self, q, fn, reads=(), writes=()):
        self._deps(q, reads, writes)
        j = self.drr[q]
        self.drr[q] = (j + 1) % len(self.dsem[q])
        sem = self.dsem[q][j]
        if self.dcnt[q][j] > 0:
            self._wait(q, sem, 16 * self.dcnt[q][j])
        self.dcnt[q][j] += 1
        tok = (sem, 16 * self.dcnt[q][j], q)
        self.streams[q].append(('o', fn, sem, 16))
        self._commit(tok, reads, writes)

    def barrier(self, engines=('pe', 'dve', 'act', 'sp')):
        for eng in engines:
            for e in self.COMPUTE:
                if self.ecnt[e] > 0 and e != eng:
                    self._wait(eng, self.esem[e], self.ecnt[e])
            for q in self.DMAQ:
                if q == 'pool':
                    continue
                for j, sem in enumerate(self.dsem[q]):
                    if self.dcnt[q][j] > 0:
                        self._wait(eng, sem, 16 * self.dcnt[q][j])

    def finish(self):
        for q in self.DMAQ:
            for j, sem in enumerate(self.dsem[q]):
                if self.dcnt[q][j] > 0:
                    self._wait('sp', sem, 16 * self.dcnt[q][j])
        for e in self.COMPUTE:
            if self.ecnt[e] > 0:
                self._wait('sp', self.esem[e], self.ecnt[e])

    def replay(self, block):
        def run(name):
            def f(e):
                for it in self.streams[name]:
                    if it[0] == 'w':
                        e.wait_ge(it[1], it[2])
                    else:
                        ins = it[1](e)
                        ins.then_inc(it[2], it[3])
            return f
        block.tensor(run('pe'))
        block.vector(run('dve'))
        block.scalar(run('act'))
        block.gpsimd(run('pool'))
        block.sync(run('sp'))


def ACT(out, in_, func, **kw):
    return lambda e: e.activation(out=out, in_=in_, func=func, **kw)


def TT(out, in0, in1, op):
    return lambda e: e.tensor_tensor(out=out, in0=in0, in1=in1, op=op)


def TS(out, in0, s1, s2, op0, op1=None):
    if op1 is None:
        return lambda e: e.tensor_scalar(out=out, in0=in0, scalar1=s1, scalar2=None, op0=op0)
    return lambda e: e.tensor_scalar(out=out, in0=in0, scalar1=s1, scalar2=s2, op0=op0, op1=op1)


def STT(out, in0, scalar, in1, op0, op1):
    return lambda e: e.scalar_tensor_tensor(out=out, in0=in0, scalar=scalar, in1=in1, op0=op0, op1=op1)


def CP(out, in_):
    return lambda e: e.tensor_copy(out=out, in_=in_)


def ACP(out, in_):
    return lambda e: e.activation(out=out, in_=in_, func=AF.Identity)


def DMA(out, in_):
    return lambda e: e.dma_start(out=out, in_=in_)


def MM(out, pairs):
    def fn(e):
        n = len(pairs)
        ins = None
        for i, (l, r) in enumerate(pairs):
            ins = e.matmul(out, lhsT=l, rhs=r, start=(i == 0), stop=(i == n - 1))
        return ins
    return fn


def MM1(out, lhsT, rhs, start, stop):
    return lambda e: e.matmul(out, lhsT=lhsT, rhs=rhs, start=start, stop=stop)


def MMS(items):
    def fn(e):
        ins = None
        for (o, l, r) in items:
            ins = e.matmul(o, lhsT=l, rhs=r, start=True, stop=True)
        return ins
    return fn


def TRS(items, ident):
    def fn(e):
        ins = None
        for (o, i) in items:
            ins = e.transpose(o, i, ident)
        return ins
    return fn


class _Stop(Exception):
    pass


def build_program(stop=None):
    nc = bass.Bass("TRN2", target_bir_lowering=False)

    def phase_end(name):
        if stop == name:
            raise _Stop()
    es = ExitStack()

    def din(n, s):
        return nc.dram_tensor(n, list(s), F32, kind="ExternalInput").ap()

    def dout(n, s):
        return nc.dram_tensor(n, list(s), F32, kind="ExternalOutput").ap()

    def sb(n, s, d=F32):
        return es.enter_context(nc.sbuf_tensor(n, list(s), d))

    x_d = din("x", [T, D])
    cvec_d = din("cvec", [128, 16])
    kctx_d = din("kctx", [512, 256])
    vctx_d = din("vctx", [512, 256])
    s0f_d = din("s0f", [8, 256, 512])
    s0b_d = din("s0b", [8, 256, 512])
    keep_d = din("keep", [128, 1])
    amask_d = din("amask", [128, 2, 896])
    ropeC_d = din("ropeC", [128, T])
    ropeS_d = din("ropeS", [128, T])
    ident_d = din("ident", [128, 128])
    perm_d = din("perm", [128, 128])
    retc_d = din("retc", [128, 4, 128])
    iota_d = din("iotarep", [128, 2, 512])
    cols_d = din("colsc", [128, 4])
    adaw_d = din("ada_w", [2, D, 6 * D])
    adab_d = din("ada_bT", [128, 2, 96])
    nmg_d = din("nmgT", [128, 2, 16])
    nfg_d = din("nfgT", [128, 2, 16])
    fing_d = din("fingT", [128, 16])
    ewin_d = din("even_w_in", [D, 3584])
    ewout_d = din("even_w_out", [D, D])
    sink_d = din("sinkR", [128, 16])
    lng_d = din("lngR", [128, 1024])
    lnb_d = din("lnbR", [128, 1024])
    wsT_d = din("sguwT", [128, 8, 128])
    bsT_d = din("sgubT", [128, 8])
    rwin_d = din("ret_w_in", [D, 12288])
    rwout_d = din("ret_w_out", [4096, D])
    gng_d = din("gngR", [128, 4096])
    decf_d = din("decfR", [128, 8])
    decb_d = din("decbR", [128, 8])
    wg_d = din("ffn_w_gate", [2, D, DFF])
    wu_d = din("ffn_w_up", [2, D, DFF])
    wd_d = din("ffn_w_down", [2, DFF, D])

    y_d = dout("y", [T, D])
    okv_d = dout("okv", [T, 512])
    osf_d = dout("osf", [4, 8, 256, 512])
    osb_d = dout("osb", [4, 8, 256, 512])

    xT = sb("xT", [128, KT, T], F32)
    hT = sb("hT", [128, KT, T], BF16)
    NSLOT = 3
    SLOT = 4096
    wbuf = sb("wbuf", [128, NSLOT, SLOT], BF16)
    sqb = sb("sqb", [128, 3, 512], BF16)
    tmpf = sb("tmpf", [128, 3, 512], F32)
    rstd = sb("rstd", [128, 512], F32)
    rstd2 = sb("rstd2", [128, 512], F32)
    identf = sb("identf", [128, 128], F32)
    identb = sb("identb", [128, 128], BF16)
    onesb = sb("onesb", [128, 128], BF16)
    permb = sb("permb", [128, 128], BF16)
    modT = sb("modT", [128, 2, 96], F32)
    adab = sb("adab", [128, 2, 96], F32)
    nmg = sb("nmg", [128, 2, 16], F32)
    nfg = sb("nfg", [128, 2, 16], F32)
    fing = sb("fing", [128, 16], F32)
    G1 = sb("G1", [128, 2, 16], F32)
    G2 = sb("G2", [128, 2, 16], F32)
    cvec = sb("cvec_s", [128, 16], F32)
    scb = sb("scb", [128, 16], BF16)
    colsc = sb("colsc_s", [128, 4], F32)
    epsc = sb("epsc", [128, 1], F32)
    keep = sb("keep_s", [128, 1], F32)
    small = sb("small", [128, 64], F32)
    stt = sb("stt", [128, 2, 6], F32)
    vcb = sb("vcb", [128, 4, 256], BF16)
    small2 = sb("small2", [128, 32], F32)
    SCR = 68 * 1024
    scr = sb("scr", [128, SCR // 4], F32)

    psF = es.enter_context(nc.psum_tensor("psF", [128, 6, 512], F32))
    psB = es.enter_context(nc.psum_tensor("psB", [128, 2, 1024], BF16))

    S = Sched(nc, es)

    def carve(off, shape, dt):
        n = 1
        for s_ in shape:
            n *= s_
        nbytes = n * (4 if dt == F32 else 2)
        assert off % 4 == 0 and off + nbytes <= SCR, (off, nbytes)
        v = scr[:, off // 4:(off + nbytes + 3) // 4]
        if dt != F32:
            v = v.bitcast(BF16)
        if len(shape) == 2:
            v = v.rearrange("p (a b) -> p a b", b=shape[1])
        elif len(shape) == 3:
            v = v.rearrange("p (a b c) -> p a b c", b=shape[1], c=shape[2])
        return v

    wst = {'i': 0}
    ring = {'slots': None}

    def base_ring():
        return [(wbuf[:, i_, :], ('w', i_)) for i_ in range(NSLOT)]

    def set_ring(slots):
        ring['slots'] = slots
        wst['i'] = 0

    set_ring(base_ring())

    def next_slot():
        sl_ = ring['slots'][wst['i'] % len(ring['slots'])]
        wst['i'] += 1
        return sl_

    def wsrc(W2d, r0, ktn, c0, ncols):
        return W2d[r0:r0 + ktn * 128, c0:c0 + ncols].rearrange("(k p) n -> p k n", p=128)

    def load_w(src, ktn, ncols):
        buf, key = next_slot()
        view = buf[:, 0:ktn * ncols].rearrange("p (k n) -> p k n", n=ncols)
        S.dma('pool', DMA(view, src), writes=[key])
        return view, key

    bst = {'i': 0}

    def nbank():
        b = bst['i'] % 5
        bst['i'] += 1
        return b

    rot = {'sq': 0, 'tf': 0}

    def nsq():
        q = rot['sq'] % 3
        rot['sq'] += 1
        return q

    def ntf():
        q = rot['tf'] % 3
        rot['tf'] += 1
        return q

    def hkeys(g):
        return [('hT', kt, g) for kt in range(KT)]

    S.dma('sp', DMA(identf[:], ident_d), writes=['identf'])
    S.dma('pool', DMA(identb[:], ident_d), writes=['identb'])
    S.dma('pool', DMA(permb[:], perm_d), writes=['permb'])
    S.dma('sp', DMA(cvec[:], cvec_d), writes=['cvec'])
    S.dma('sp', DMA(adab[:], adab_d), writes=['adab'])
    S.dma('sp', DMA(nmg[:], nmg_d), writes=['nmg'])
    S.dma('sp', DMA(nfg[:], nfg_d), writes=['nfg'])
    S.dma('sp', DMA(fing[:], fing_d), writes=['fing'])
    S.dma('sp', DMA(colsc[:], cols_d), writes=['colsc'])
    S.dma('sp', DMA(keep[:], keep_d), writes=['keep'])
    S.dma('pool', DMA(vcb[:], vctx_d.rearrange("(b p) c -> p b c", p=128)), writes=['vcb'])
    S.op('dve', lambda e: e.memset(onesb[:], 1.0), writes=['onesb'])
    S.op('dve', lambda e: e.memset(epsc[:], EPS), writes=['epsc'])
    S.op('act', ACT(scb[:], cvec[:], AF.Silu), reads=['cvec'], writes=['sc'])

    ADAB = 5

    def ada_slab(l, sl):
        wv, wk = load_w(wsrc(adaw_d[l], 0, 16, sl * 256, 256), 16, 256)
        for j in range(2):
            n = sl * 2 + j
            S.op('pe', MM(psF[:, ADAB, n:n + 1], [(wv[:, kt, j * 128:(j + 1) * 128], scb[:, kt:kt + 1]) for kt in range(16)]),
                 reads=[wk, 'sc'], writes=[('ps', ADAB)])

    def ada_final(l, n0, n1):
        S.op('dve', TT(modT[:, l, n0:n1], psF[:, ADAB, n0:n1], adab[:, l, n0:n1], ALU.add), reads=[('ps', ADAB), 'adab'], writes=[('mod', l)])
        if n0 <= 16 and n1 >= 32:
            S.op('dve', STT(G1[:, l, :], modT[:, l, 16:32], 1.0, nmg[:, l, :], ALU.add, ALU.mult), reads=[('mod', l), 'nmg'], writes=[('G1', l)])
        if n0 <= 64 and n1 >= 80:
            S.op('dve', STT(G2[:, l, :], modT[:, l, 64:80], 1.0, nfg[:, l, :], ALU.add, ALU.mult), reads=[('mod', l), 'nfg'], writes=[('G2', l)])

    class _Ticker:
        def __init__(self, l, slabs):
            self.slabs = [(l, sl_) for sl_ in slabs]

        def tick(self, n=1):
            for _ in range(n):
                if self.slabs:
                    it = self.slabs.pop(0)
                    if callable(it):
                        it()
                        self.tick()
                    else:
                        ada_slab(it[0], it[1])

        def flush(self):
            self.tick(len(self.slabs))

    def norm_to_hT(Gc, Bc, gkeys):
        rbuf = [(rstd, 'rstd'), (rstd2, 'rstd2')]
        for g in range(2):
            gs = slice(g * 512, (g + 1) * 512)
            rb, rkey = rbuf[g]
            pb = nbank()
            for kt in range(KT):
                q = nsq()
                if kt % 2 == 0:
                    S.op('act', ACT(sqb[:, q, :], xT[:, kt, gs], AF.Square), reads=[('xT', kt, g)], writes=[('sq', q)])
                else:
                    S.op('dve', TT(sqb[:, q, :], xT[:, kt, gs], xT[:, kt, gs], ALU.mult), reads=[('xT', kt, g)], writes=[('sq', q)])
                S.op('pe', MM1(psF[:, pb, :], onesb[:], sqb[:, q, :], kt == 0, kt == KT - 1),
                     reads=[('sq', q), 'onesb'], writes=[('ps', pb)])
            S.op('act', ACT(rb[:], psF[:, pb, :], AF.Sqrt, scale=1.0 / D, bias=epsc[:, 0:1]), reads=[('ps', pb), 'epsc'], writes=[rkey])
            S.op('dve', (lambda rb: (lambda e: e.reciprocal(out=rb[:], in_=rb[:])))(rb), reads=[rkey], writes=[rkey])
        for g in range(2):
            gs = slice(g * 512, (g + 1) * 512)
            rb, rkey = rbuf[g]
            for kt in range(KT):
                q = ntf()
                S.op('dve', TT(tmpf[:, q, :], xT[:, kt, gs], rb[:], ALU.mult), reads=[('xT', kt, g), rkey], writes=[('tf', q)])
                S.op('act', ACT(hT[:, kt, gs], tmpf[:, q, :], AF.Identity, scale=Gc[:, kt:kt + 1], bias=Bc[:, kt:kt + 1]),
                     reads=[('tf', q)] + gkeys, writes=[('hT', kt, g)])

    def resid_gemm(slabs, rhs_fn, rhs_keys_fn, gate, gkey, ticker=None):
        for (src, ktn, ncols, n0) in slabs:
            wv, wk = load_w(src, ktn, ncols)
            for j in range(ncols // 128):
                n = n0 + j
                for g in range(2):
                    gs = slice(g * 512, (g + 1) * 512)
                    pb = nbank()
                    S.op('pe', MM(psF[:, pb, :], [(wv[:, k, j * 128:(j + 1) * 128], rhs_fn(k, gs)) for k in range(ktn)]),
                         reads=[wk] + rhs_keys_fn(g), writes=[('ps', pb)])
                    S.op('dve', STT(xT[:, n, gs], psF[:, pb, :], gate[:, n:n + 1], xT[:, n, gs], ALU.mult, ALU.add),
                         reads=[('ps', pb), gkey, ('xT', n, g)], writes=[('xT', n, g)])
            if ticker is not None:
                ticker.tick()

    def ffn(l, ticker=None):
        S.barrier(engines=('pool',))
        set_ring(base_ring() + [(carve(45056 + i_ * 8192, [4096], BF16), ('wx', i_)) for i_ in range(3)])
        norm_to_hT(G2[:, l, :], modT[:, l, 48:64], [('G2', l), ('mod', l)])
        actT = carve(0, [22, T], BF16)

        for half in range(2):
            f0 = half * 22
            for sp_ in range(11):
                c0 = (f0 + sp_ * 2) * 128
                wvg, wkg = load_w(wsrc(wg_d[l], 0, 16, c0, 256), 16, 256)
                wvu, wku = load_w(wsrc(wu_d[l], 0, 16, c0, 256), 16, 256)
                for j in range(2):
                    fl = sp_ * 2 + j
                    for g in range(2):
                        gs = slice(g * 512, (g + 1) * 512)
                        pg = nbank()
                        S.op('pe', MM(psF[:, pg, :], [(wvg[:, kt, j * 128:(j + 1) * 128], hT[:, kt, gs]) for kt in range(KT)]),
                             reads=[wkg] + hkeys(g), writes=[('ps', pg)])
                        pu = nbank()
                        S.op('pe', MM(psF[:, pu, :], [(wvu[:, kt, j * 128:(j + 1) * 128], hT[:, kt, gs]) for kt in range(KT)]),
                             reads=[wku] + hkeys(g), writes=[('ps', pu)])
                        q = nsq()
                        S.op('act', ACT(sqb[:, q, :], psF[:, pg, :], AF.Silu), reads=[('ps', pg)], writes=[('sq', q)])
                        S.op('dve', TT(actT[:, fl, gs], psF[:, pu, :], sqb[:, q, :], ALU.mult),
                             reads=[('ps', pu), ('sq', q)], writes=[('actT', fl, g)])
                if ticker is not None:
                    ticker.tick()
            slabs = [(wsrc(wd_d[l], f0 * 128, 22, n * 128, 128), 22, 128, n) for n in range(16)]
            resid_gemm(slabs, lambda k, gs: actT[:, k, gs], lambda g: [('actT', k, g) for k in range(22)],
                       modT[:, l, 80:96], ('mod', l), ticker=ticker)
        if ticker is not None:
            ticker.flush()
        set_ring(base_ring())

    try:
        xin = carve(0, [2, D], F32)
        set_ring(base_ring() + [(carve(45056 + i_ * 8192, [4096], BF16), ('wx', i_)) for i_ in range(3)])
        for b in range(8):
            S.dma('sp', DMA(xin[:, b % 2, :], x_d[b * 128:(b + 1) * 128, :]), writes=[('xin', b % 2)])
            for k4 in range(4):
                pb = nbank()
                S.op('pe', TRS([(psF[:, pb, j * 128:(j + 1) * 128], xin[:, b % 2, (k4 * 4 + j) * 128:(k4 * 4 + j + 1) * 128]) for j in range(4)], identf[:]),
                     reads=[('xin', b % 2), 'identf'], writes=[('ps', pb)])
                S.op('dve' if k4 % 2 else 'act',
                     (CP if k4 % 2 else ACP)(xT[:, k4 * 4:(k4 + 1) * 4, b * 128:(b + 1) * 128], psF[:, pb, :].rearrange("p (k t) -> p k t", t=128)),
                     reads=[('ps', pb)], writes=[('xT', k4 * 4 + j, b // 4) for j in range(4)])
            ada_slab(0, 2 * b)
            ada_slab(0, 2 * b + 1)
        set_ring(base_ring())
        ada_final(0, 0, 32)
        tk0 = _Ticker(0, range(16, 24))
        tk0b = _Ticker(0, range(24, 40))
        phase_end('ada0')
        S.barrier()
        phase_end('xin')

        norm_to_hT(G1[:, 0, :], modT[:, 0, 0:16], [('G1', 0), ('mod', 0)])
        phase_end('l0norm')

        qT = carve(0, [8, T], BF16)
        kT = carve(16384, [4, T], BF16)
        vtok = carve(24576, [8, 256], BF16)
        ubf = carve(28672, [8, 1024], BF16)
        vgbf = carve(45056, [8, 1024], BF16)
        TMP0 = 61440
        ropeC = carve(TMP0, [T], F32)
        ropeS = carve(TMP0 + 4096, [T], F32)
        S.dma('sp', DMA(ropeC, ropeC_d), writes=['ropeC'])
        S.dma('sp', DMA(ropeS, ropeS_d), writes=['ropeS'])

        def rope_evac(pb, g, dst, dkey):
            gs = slice(g * 512, (g + 1) * 512)
            q1 = ntf()
            S.op('dve', TT(tmpf[:, q1, :], psF[:, pb, :], ropeC[:, gs], ALU.mult), reads=[('ps', pb), 'ropeC'], writes=[('tf', q1)])
            qb_ = nsq()
            S.op('act', ACP(sqb[:, qb_, :], psF[:, pb, :]), reads=[('ps', pb), ('tf', q1)], writes=[('sq', qb_)])
            pb2 = nbank()
            S.op('pe', MM(psF[:, pb2, :], [(permb[:], sqb[:, qb_, :])]), reads=[('sq', qb_), 'permb'], writes=[('ps', pb2)])
            q2 = ntf()
            S.op('dve', TT(tmpf[:, q2, :], psF[:, pb2, :], ropeS[:, gs], ALU.mult), reads=[('ps', pb2), 'ropeS'], writes=[('tf', q2)])
            S.op('dve', TT(dst, tmpf[:, q1, :], tmpf[:, q2, :], ALU.add), reads=[('tf', q1), ('tf', q2)], writes=[dkey])

        for sl in range(4):
            wv, wk = load_w(wsrc(ewin_d, 0, 16, sl * 256, 256), 16, 256)
            for j in range(2):
                tl = sl * 2 + j
                for g in range(2):
                    gs = slice(g * 512, (g + 1) * 512)
                    pb = nbank()
                    S.op('pe', MM(psF[:, pb, :], [(wv[:, kt, j * 128:(j + 1) * 128], hT[:, kt, gs]) for kt in range(KT)]),
                         reads=[wk] + hkeys(g), writes=[('ps', pb)])
                    rope_evac(pb, g, qT[:, tl, gs], ('qT', tl, g))
            tk0.tick(1)
        phase_end('l0q')
        for sl in range(2):
            buf_, wk = next_slot()
            wv = buf_[:, 0:16 * 256].rearrange("p (k n) -> p k n", n=256)
            for j in range(2):
                kvh = sl * 2 + j
                for rep in range(2):
                    S.dma('pool', DMA(wv[:, :, j * 128 + rep * 64:j * 128 + rep * 64 + 64], wsrc(ewin_d, 0, 16, 1024 + kvh * 64, 64)), writes=[wk])
            for j in range(2):
                kvh = sl * 2 + j
                for g in range(2):
                    gs = slice(g * 512, (g + 1) * 512)
                    pb = nbank()
                    S.op('pe', MM(psF[:, pb, :], [(wv[:, kt, j * 128:(j + 1) * 128], hT[:, kt, gs]) for kt in range(KT)]),
                         reads=[wk] + hkeys(g), writes=[('ps', pb)])
                    rope_evac(pb, g, kT[:, kvh, gs], ('kT', kvh, g))
            tk0.tick(1)

        phase_end('l0projB')
        okst = rstd[:].rearrange("p (a b) -> p a b", b=256)
        okc = {'i': 0}
        for sl in range(10):
            c0 = 1024 + sl * 256
            wv, wk = load_w(wsrc(ewin_d, 0, 16, c0, 256), 16, 256)
            for b in range(8):
                bs = slice(b * 128, (b + 1) * 128)
                pb = nbank()
                S.op('pe', MM(psF[:, pb, 0:256], [(hT[:, kt, bs], wv[:, kt, :]) for kt in range(KT)]),
                     reads=[wk] + hkeys(b // 4), writes=[('ps', pb)])
                if sl < 2:
                    q = okc['i'] % 2
                    okc['i'] += 1
                    S.op('act', ACP(okst[:, q, :], psF[:, pb, 0:256]), reads=[('ps', pb)], writes=[('okst', q), 'rstd'])
                    S.dma('sp', DMA(okv_d[bs, sl * 256:(sl + 1) * 256], okst[:, q, :]), reads=[('okst', q)])
                    if sl == 1:
                        S.op('dve', CP(vtok[:, b, :], psF[:, pb, 0:256]), reads=[('ps', pb)], writes=[('vtok', b)])
                elif sl < 6:
                    cc = (sl - 2) * 256
                    S.op('act', ACT(ubf[:, b, cc:cc + 256], psF[:, pb, 0:256], AF.Gelu_apprx_tanh), reads=[('ps', pb)], writes=[('ubf', b, sl)])
                else:
                    cc = (sl - 6) * 256
                    S.op('act', ACT(vgbf[:, b, cc:cc + 256], psF[:, pb, 0:256], AF.Gelu_apprx_tanh), reads=[('ps', pb)], writes=[('vgbf', b, sl)])
            tk0.tick(1)
        tk0.flush()
        ada_final(0, 32, 48)
        S.barrier()
        phase_end('l0proj')

        lng = carve(TMP0, [1024], F32)
        lnb = carve(TMP0 + 4096, [1024], F32)
        S.dma('sp', DMA(lng, lng_d), writes=['lng'])
        S.dma('sp', DMA(lnb, lnb_d), writes=['lnb'])
        wsT = rstd[:].bitcast(BF16).rearrange("p (g q) -> p g q", q=128)
        for hf in range(2):
            S.dma('sp', DMA(tmpf[:, 2, :].rearrange("p (g q) -> p g q", q=128), wsT_d[:, hf * 4:(hf + 1) * 4, :]), writes=[('tf', 2)])
            S.op('act', ACP(wsT[:, hf * 4:(hf + 1) * 4, :], tmpf[:, 2, :].rearrange("p (g q) -> p g q", q=128)), reads=[('tf', 2)], writes=['wsT', 'rstd'])
        bsT = small[:, 0:8]
        S.dma('sp', DMA(bsT, bsT_d), writes=['bsT'])
        vn32 = tmpf[:, 0:2, :].rearrange("p a b -> p (a b)")
        vnb = sqb[:, 0:2, :].rearrange("p a b -> p (a b)")
        mvall = small[:, 32:48].rearrange("p (b t) -> p b t", t=2)
        sdall = small[:, 48:56]
        rsall = small[:, 56:64]
        for b in range(8):
            vg_b = vgbf[:, b, :]
            S.op('dve', lambda e, vg_b=vg_b: e.bn_stats(out=stt[:, 0, :], in_=vg_b[:, 0:512]), reads=[('vgbf', b, s_) for s_ in range(6, 10)], writes=['stt0'])
            S.op('dve', lambda e, vg_b=vg_b: e.bn_stats(out=stt[:, 1, :], in_=vg_b[:, 512:1024]), reads=[('vgbf', b, s_) for s_ in range(6, 10)], writes=['stt1'])
            S.op('dve', lambda e, b=b: e.bn_aggr(out=mvall[:, b, :], in_=stt[:].rearrange("p a b -> p (a b)")), reads=['stt0', 'stt1'], writes=['mv'])
        S.op('act', ACT(sdall, mvall[:, :, 1], AF.Sqrt, bias=epsc[:, 0:1]), reads=['mv', 'epsc'], writes=['sd'])
        S.op('dve', lambda e: e.reciprocal(out=rsall, in_=sdall), reads=['sd'], writes=['rs'])
        for b in range(8):
            bs = slice(b * 128, (b + 1) * 128)
            vg_b = vgbf[:, b, :]
            S.op('dve', TS(vn32, vg_b, mvall[:, b, 0:1], rsall[:, b:b + 1], ALU.subtract, ALU.mult), reads=['mv', 'rs', ('vgbf', b, 6)], writes=[('tf', 0), ('tf', 1)])
            S.op('dve', TT(vn32, vn32, lng, ALU.mult), reads=[('tf', 0), 'lng'], writes=[('tf', 0), ('tf', 1)])
            S.op('dve', TT(vnb, vn32, lnb, ALU.add), reads=[('tf', 0), 'lnb'], writes=[('sq', 0), ('sq', 1)])
            S.op('pe', MMS([(psF[:, g_ // 4, (g_ % 4) * 128:(g_ % 4 + 1) * 128], wsT[:, g_, :], vnb[:, g_ * 128:(g_ + 1) * 128]) for g_ in range(8)]),
                 reads=[('sq', 0), ('sq', 1), 'wsT'], writes=[('ps', 0), ('ps', 1)])
            for g_ in range(8):
                S.op('dve', STT(vg_b[:, g_ * 128:(g_ + 1) * 128], psF[:, g_ // 4, (g_ % 4) * 128:(g_ % 4 + 1) * 128], bsT[:, g_:g_ + 1],
                                ubf[:, b, g_ * 128:(g_ + 1) * 128], ALU.add, ALU.mult),
                     reads=[('ps', g_ // 4), 'bsT'] + [('ubf', b, s_) for s_ in range(2, 6)], writes=[('gout', b)])
            par = b % 2
            S.op('pe', TRS([(psB[:, par, j * 128:(j + 1) * 128], vg_b[:, j * 128:(j + 1) * 128]) for j in range(8)], identb[:]),
                 reads=[('gout', b), 'identb'], writes=[('pb', par)])
            S.op('act', ACP(hT[:, 8:16, bs], psB[:, par, :].rearrange("p (k t) -> p k t", t=128)), reads=[('pb', par)],
                 writes=[('hT', 8 + j, b // 4) for j in range(8)])
            tk0b.tick(2)
        tk0b.flush()
        ada_final(0, 48, 80)
        S.barrier()
        phase_end('sgu')

        A0 = 28672
        kcT = carve(A0, [4, 512], BF16)
        amask = carve(A0 + 4096, [2, 896], BF16)
        Pb = carve(A0 + 8192, [2, 896], BF16)
        PTb = carve(A0 + 12288, [2, 896], BF16)
        atok = carve(A0 + 16384, [2, 1024], BF16)
        kcd = carve(A0 + 20480, [2, 128], BF16)
        sinkT = small[:, 16:32]
        am32 = carve(A0 + 25600, [2, 896], F32)
        kc32 = carve(A0 + 21504, [4, 256], F32)
        S.dma('sp', DMA(am32, amask_d), writes=['am32'])
        S.op('dve', CP(amask, am32), reads=['am32'], writes=['amask'])
        S.dma('sp', DMA(kc32, kctx_d.rearrange("(b p) c -> p b c", p=128)), writes=['kcb'])
        S.dma('sp', DMA(sinkT, sink_d), writes=['sinkT'])
        cnt = 0
        for kvh in range(4):
            for blk in range(4):
                q = cnt % 2
                cnt += 1
                S.op('dve', CP(kcd[:, q, :].rearrange("p (r d) -> p r d", r=2),
                               kc32[:, blk, kvh * 64:(kvh + 1) * 64].unsqueeze(1).broadcast_to([128, 2, 64])),
                     reads=['kcb'], writes=[('kcd', q)])
                S.op('pe', TRS([(psB[:, q, 0:128], kcd[:, q, :])], identb[:]), reads=[('kcd', q), 'identb'], writes=[('pb', q)])
                S.op('act', ACP(kcT[:, kvh, blk * 128:(blk + 1) * 128], psB[:, q, 0:128]), reads=[('pb', q)], writes=['kcT'])

        phase_end('kct')
        negm = small[:, 32:48]
        rsum = small[:, 48:64]
        mxc = small2[:, 16:17]
        sk = small2[:, 0:16]
        for i in range(8):
            isl = slice(i * 128, (i + 1) * 128)
            lo = max(i - 1, 0)
            hi = min(i + 1, 7)
            nloc = (hi - lo + 1) * 128
            offlo = lo - (i - 1)
            par = i % 2
            c0 = 512 - nloc
            W_ = nloc + 512
            nblk = W_ // 128

            def s_op(h, isl=isl, lo=lo, hi=hi, nloc=nloc, offlo=offlo, par=par, c0=c0, i=i):
                sp_ = h % 2
                hb = (h % 2) * 64
                tl = h // 2
                kvh = h // 4
                Sv = psF[:, 2 * sp_:2 * sp_ + 2, :].rearrange("p a b -> p (a b)")
                o1 = Sv[:, c0:512]
                o2 = Sv[:, 512:1024]
                l1 = qT[hb:hb + 64, tl, isl]
                r1 = kT[hb:hb + 64, kvh, lo * 128:(hi + 1) * 128]
                r2 = amask[:, par, offlo * 128:offlo * 128 + nloc]
                r3 = kcT[hb:hb + 64, kvh, :]
                r4 = amask[:, par, 384:896]
                idb = identb[:]

                def fn(e):
                    e.matmul(o1, lhsT=l1, rhs=r1, start=True, stop=False)
                    e.matmul(o1, lhsT=idb, rhs=r2, start=False, stop=True)
                    e.matmul(o2, lhsT=l1, rhs=r3, start=True, stop=False)
                    return e.matmul(o2, lhsT=idb, rhs=r4, start=False, stop=True)
                S.op('pe', fn, reads=[('qT', tl, i // 4), ('kT', kvh, 0), ('kT', kvh, 1), 'kcT', 'amask', 'identb'],
                     writes=[('ps', 2 * sp_), ('ps', 2 * sp_ + 1)])

            S.op('dve', lambda e: e.memset(rsum, 0.0), writes=[('rsum', h_) for h_ in range(16)])
            def emit_max(h, c0=c0):
                sp_ = h % 2
                Sv = psF[:, 2 * sp_:2 * sp_ + 2, :].rearrange("p a b -> p (a b)")
                S.op('dve', lambda e, Sv=Sv, c0=c0: e.reduce_max(out=mxc, in_=Sv[:, c0:1024], axis=mybir.AxisListType.X),
                     reads=[('ps', 2 * sp_), ('ps', 2 * sp_ + 1)], writes=['mxc'])
                S.op('dve', TS(negm[:, h:h + 1], mxc, -0.125, None, ALU.mult), reads=['mxc'], writes=[('negm', h)])

            def emit_exp(h, c0=c0, W_=W_):
                sp_ = h % 2
                Sv = psF[:, 2 * sp_:2 * sp_ + 2, :].rearrange("p a b -> p (a b)")
                S.op('act', ACT(Pb[:, sp_, 0:W_], Sv[:, c0:1024], AF.Exp, scale=0.125, bias=negm[:, h:h + 1], accum_out=rsum[:, h:h + 1]),
                     reads=[('ps', 2 * sp_), ('ps', 2 * sp_ + 1), ('negm', h)], writes=[('P', sp_), ('rsum', h)])

            def emit_T(h, nblk=nblk):
                sp_ = h % 2
                S.op('pe', TRS([(psB[:, sp_, kb * 128:(kb + 1) * 128], Pb[:, sp_, kb * 128:(kb + 1) * 128]) for kb in range(nblk)], identb[:]),
                     reads=[('P', sp_), 'identb'], writes=[('pb', sp_)])

            def emit_evac(h, W_=W_):
                sp_ = h % 2
                S.op('dve', CP(PTb[:, sp_, 0:W_], psB[:, sp_, 0:W_]), reads=[('pb', sp_)], writes=[('PT', sp_)])

            def emit_PV(h, nblk=nblk, lo=lo, hi=hi):
                sp_ = h % 2
                kvh = h // 4
                Ov = psF[:, 4:6, :].rearrange("p a b -> p (a b)")
                pairs = []
                for kb in range(nblk):
                    if kb < nblk - 4:
                        vv = vtok[:, lo + kb, kvh * 64:(kvh + 1) * 64]
                    else:
                        vv = vcb[:, kb - (nblk - 4), kvh * 64:(kvh + 1) * 64]
                    pairs.append((PTb[:, sp_, kb * 128:(kb + 1) * 128], vv))
                S.op('pe', MM(Ov[:, h * 64:(h + 1) * 64], pairs), reads=[('PT', sp_), 'vcb'] + [('vtok', b_) for b_ in range(lo, hi + 1)],
                     writes=[('ps', 4), ('ps', 5)])

            s_op(0)
            s_op(1)
            emit_max(0)
            emit_exp(0)
            for h in range(16):
                if h + 1 < 16:
                    emit_max(h + 1)
                    emit_exp(h + 1)
                emit_T(h)
                if h + 2 < 16:
                    s_op(h + 2)
                emit_evac(h)
                emit_PV(h)
            allh = [('negm', h) for h in range(16)]
            S.op('dve', TT(sk, sinkT, negm, ALU.add), reads=allh + ['sinkT'], writes=['sk'])
            S.op('act', ACT(sk, sk, AF.Exp), reads=['sk'], writes=['sk'])
            S.op('dve', TT(sk, sk, rsum, ALU.add), reads=['sk'] + [('rsum', h) for h in range(16)], writes=['sk'])
            S.op('dve', lambda e: e.reciprocal(out=sk, in_=sk), reads=['sk'], writes=['sk'])
            Ov = psF[:, 4:6, :].rearrange("p a b -> p (a b)")
            S.op('dve', TT(atok[:, par, :].rearrange("p (h d) -> p h d", d=64), Ov.rearrange("p (h d) -> p h d", d=64),
                           sk.unsqueeze(2).broadcast_to([128, 16, 64]), ALU.mult),
                 reads=['sk', ('ps', 4), ('ps', 5)], writes=[('atok', par)])
            S.op('pe', TRS([(psB[:, par, j * 128:(j + 1) * 128], atok[:, par, j * 128:(j + 1) * 128]) for j in range(8)], identb[:]),
                 reads=[('atok', par), 'identb'], writes=[('pb', par)])
            S.op('act', ACP(hT[:, 0:8, isl], psB[:, par, :].rearrange("p (k t) -> p k t", t=128)), reads=[('pb', par)],
                 writes=[('hT', j, i // 4) for j in range(8)])

        phase_end('attn')
        slabs = [(wsrc(ewout_d, 0, 16, sl * 256, 256), 16, 256, sl * 2) for sl in range(8)]
        resid_gemm(slabs, lambda k, gs: hT[:, k, gs], hkeys, modT[:, 0, 32:48], ('mod', 0))
        S.barrier()
        phase_end('l0out')

        tk1 = _Ticker(0, range(40, 48))
        tk1.slabs.append(lambda: ada_final(0, 80, 96))
        tk1.slabs.extend((1, sl_) for sl_ in range(48))
        ffn(0, tk1)
        phase_end('ffn0')
        tk1.flush()
        ada_final(1, 0, 96)
        S.barrier()
        phase_end('ffn0ada1')

        norm_to_hT(G1[:, 1, :], modT[:, 1, 0:16], [('G1', 1), ('mod', 1)])
        rq = carve(0, [2, T], BF16)
        rk = carve(4096, [2, T], BF16)
        y2T = carve(0, [4, T], BF16)
        qsf = carve(8192, [2, T], BF16)
        qsb = carve(12288, [2, T], BF16)
        Kf = carve(16384, [8, 256], BF16)
        Kb = carve(20480, [8, 256], BF16)
        vv_ = carve(24576, [8, 512], BF16)
        gg = carve(32768, [8, 512], BF16)
        Sbh = carve(40960, [8, 2, 512], BF16)
        S32 = carve(57344, [2, 512], F32)
        Sout = tmpf[:, 0:2, :]
        Sfb = carve(61440, [2, 512], BF16)
        sTall = carve(63488, [8, 128], BF16)
        Mh = carve(65536, [128], F32)
        m2 = carve(66048, [128], F32)
        retcs = carve(66560, [4, 128], BF16)
        iotas = carve(67584, [2, 512], BF16)
        gnrow = rstd
        lfT = small[:, 0:8]
        lbT = small[:, 8:16]
        koutf = small[:, 16:24]
        koutb = small[:, 24:32]
        gchf = small[:, 32:40]
        gchb = small[:, 40:48]
        tmp8 = small[:, 48:56]
        decf = small[:, 56:64]
        S.dma('sp', DMA(decf, decf_d), writes=['decf'])
        S.op('act', ACT(tmp8, decf, AF.Exp, scale=-1.0), reads=['decf'], writes=['tmp8'])
        S.op('act', ACT(tmp8, tmp8, AF.Ln, bias=colsc[:, 3:4]), reads=['tmp8', 'colsc'], writes=['tmp8'])
        S.op('dve', TS(lfT, tmp8, -1.0, None, ALU.mult), reads=['tmp8'], writes=['lfT'])
        S.dma('sp', DMA(decf, decb_d), reads=['tmp8'], writes=['decf'])
        S.op('act', ACT(tmp8, decf, AF.Exp, scale=-1.0), reads=['decf', 'lfT'], writes=['tmp8'])
        S.op('act', ACT(tmp8, tmp8, AF.Ln, bias=colsc[:, 3:4]), reads=['tmp8', 'colsc'], writes=['tmp8'])
        S.op('dve', TS(lbT, tmp8, -1.0, None, ALU.mult), reads=['tmp8'], writes=['lbT'])
        for (dst, src, col, scl, key) in ((koutf, lfT, 0, 1.0 / 16, 'koutf'), (koutb, lbT, 1, 1.0 / 16, 'koutb'),
                                          (gchf, lfT, 2, 1.0, 'gchf'), (gchb, lbT, 2, 1.0, 'gchb')):
            S.op('dve', TS(dst, src, colsc[:, col:col + 1], None, ALU.mult), reads=['lfT', 'lbT', 'colsc'], writes=[key])
            S.op('act', ACT(dst, dst, AF.Exp), reads=[key], writes=[key])
            if scl != 1.0:
                S.op('dve', TS(dst, dst, scl, None, ALU.mult), reads=[key], writes=[key])
        S.dma('sp', DMA(tmpf[:, 2, :].rearrange("p (a b) -> p a b", b=128), retc_d), writes=[('tf', 2)])
        S.op('dve', CP(retcs, tmpf[:, 2, :].rearrange("p (a b) -> p a b", b=128)), reads=[('tf', 2)], writes=['retcs'])
        S.dma('sp', DMA(tmpf[:, 0:2, :], iota_d), writes=[('tf', 0), ('tf', 1)])
        S.op('dve', CP(iotas, tmpf[:, 0:2, :]), reads=[('tf', 0), ('tf', 1)], writes=['iotas'])
        qinF = tmpf[:, 0, :]
        qinB = tmpf[:, 1, :]

        def v_unit(wv, wk, sl, b, eng):
            bs = slice(b * 128, (b + 1) * 128)
            pb = nbank()
            S.op('pe', MM(psF[:, pb, 0:256], [(hT[:, kt, bs], wv[:, kt, :]) for kt in range(KT)]),
                 reads=[wk] + hkeys(b // 4), writes=[('ps', pb)])
            if eng == 'dve':
                S.op('dve', CP(vv_[:, b, sl * 256:(sl + 1) * 256], psF[:, pb, 0:256]), reads=[('ps', pb)], writes=[('vv', b, sl)])
            else:
                S.op('act', ACP(vv_[:, b, sl * 256:(sl + 1) * 256], psF[:, pb, 0:256]), reads=[('ps', pb)], writes=[('vv', b, sl)])

        for h in range(8):
            hc = slice(h, h + 1)
            S.op('act', ACT(Mh, retcs[:, 0, :], AF.Exp, scale=lfT[:, hc]), reads=['retcs', 'lfT'], writes=['Mh'])
            S.op('dve', TT(Mh, Mh, retcs[:, 2, :], ALU.mult), reads=['Mh', 'retcs'], writes=['Mh'])
            S.op('act', ACT(m2, retcs[:, 1, :], AF.Exp, scale=lbT[:, hc]), reads=['retcs', 'lbT'], writes=['m2'])
            S.op('dve', TT(m2, m2, retcs[:, 3, :], ALU.mult), reads=['m2', 'retcs'], writes=['m2'])
            S.op('dve', TT(Mh, Mh, m2, ALU.add), reads=['Mh', 'm2'], writes=['Mh'])
            S.op('act', ACT(qinF, iotas[:, 0, :], AF.Exp, scale=lfT[:, hc]), reads=['iotas', 'lfT'], writes=[('tf', 0)])
            S.op('act', ACT(qinB, iotas[:, 1, :], AF.Exp, scale=lbT[:, hc]), reads=['iotas', 'lbT'], writes=[('tf', 1)])
            S.dma('sp', DMA(gnrow[:], gng_d[:, h * 512:(h + 1) * 512]), writes=['rstd'])
            S.dma('sp', DMA(S32, s0b_d[h].rearrange("(k p) e -> p k e", p=128)), writes=['S32'])

            wv, wk = load_w(wsrc(rwin_d, 0, 16, h * 256, 256), 16, 256)
            for j in range(2):
                for g in range(2):
                    gs = slice(g * 512, (g + 1) * 512)
                    pb = nbank()
                    S.op('pe', MM(psF[:, pb, :], [(wv[:, kt, j * 128:(j + 1) * 128], hT[:, kt, gs]) for kt in range(KT)]),
                         reads=[wk] + hkeys(g), writes=[('ps', pb)])
                    S.op('act', ACP(rq[:, j, gs], psF[:, pb, :]), reads=[('ps', pb)], writes=[('rq', j, g), ('y2T', g)])
                    S.op('dve', TT(qsf[:, j, gs], psF[:, pb, :], qinF, ALU.mult), reads=[('ps', pb), ('tf', 0)], writes=[('qsf', j, g)])
                    S.op('dve', TT(qsb[:, j, gs], psF[:, pb, :], qinB, ALU.mult), reads=[('ps', pb), ('tf', 1)], writes=[('qsb', j, g)])
            wv, wk = load_w(wsrc(rwin_d, 0, 16, 2048 + h * 256, 256), 16, 256)
            for b in range(8):
                bs = slice(b * 128, (b + 1) * 128)
                pb = nbank()
                S.op('pe', MM(psF[:, pb, 0:256], [(hT[:, kt, bs], wv[:, kt, :]) for kt in range(KT)]),
                     reads=[wk] + hkeys(b // 4), writes=[('ps', pb)])
                q = nsq()
                S.op('act', ACP(sqb[:, q, 0:256], psF[:, pb, 0:256]), reads=[('ps', pb)], writes=[('sq', q)])
                S.op('dve', TS(Kb[:, b, :], psF[:, pb, 0:256], koutb[:, hc], None, ALU.mult), reads=[('ps', pb), 'koutb'],
                     writes=[('Kb', b)] + (['SfbB'] if b < 4 else []))
                S.op('act', ACT(Kf[:, b, :], psF[:, pb, 0:256], AF.Identity, scale=koutf[:, hc]), reads=[('ps', pb), 'koutf'], writes=[('Kf', b)])
                par = b % 2
                S.op('pe', TRS([(psB[:, par, j * 128:(j + 1) * 128], sqb[:, q, j * 128:(j + 1) * 128]) for j in range(2)], identb[:]),
                     reads=[('sq', q), 'identb'], writes=[('pb', par)])
                S.op('dve' if b % 2 else 'act', (CP if b % 2 else ACP)(rk[:, 0:2, bs], psB[:, par, 0:256].rearrange("p (j t) -> p j t", t=128)),
                     reads=[('pb', par)], writes=[('rk', 0, b // 4), ('rk', 1, b // 4), ('y2T', b // 4)])
            if h == 0:
                for sl in range(2):
                    wv, wk = load_w(wsrc(rwin_d, 0, 16, 4096 + h * 512 + sl * 256, 256), 16, 256)
                    for b in range(8):
                        v_unit(wv, wk, sl, b, 'dve' if b % 2 else 'act')
            for c in range(8):
                cs = slice(c * 128, (c + 1) * 128)
                pb = nbank()
                S.op('pe', MM(psF[:, pb, 0:128], [(rk[:, j, cs], rq[:, j, cs]) for j in range(2)]),
                     reads=[('rk', j, c // 4) for j in range(2)] + [('rq', j, c // 4) for j in range(2)], writes=[('ps', pb)])
                S.op('dve', TT(sTall[:, c, :], psF[:, pb, 0:128], Mh, ALU.mult), reads=[('ps', pb), 'Mh'], writes=[('sT', c)])
            S.op('act', ACP(Sbh[:, 7, :, :], S32), reads=['S32'], writes=[('Sbh', 7)])
            def bwd_chunk(c, h=h, hc=hc):
                p0 = nbank()
                p1 = nbank()
                S.op('pe', MMS([(psF[:, p0, :], Kb[:, c, 0:128], vv_[:, c, :]), (psF[:, p1, :], Kb[:, c, 128:256], vv_[:, c, :])]),
                     reads=[('Kb', c), ('vv', c, 0), ('vv', c, 1)], writes=[('ps', p0), ('ps', p1)])
                toS = (c % 2 == 0)
                dst = Sout if toS else S32
                for j, pj in enumerate((p0, p1)):
                    S.op('dve', STT(dst[:, j, :], S32[:, j, :], gchb[:, hc], psF[:, pj, :], ALU.mult, ALU.add),
                         reads=['S32', 'gchb', ('ps', pj)], writes=[('tf', j)] if toS else ['S32'])
                if toS:
                    S.dma('sp', DMA(osb_d[c // 2, h].rearrange("(k p) e -> p k e", p=128), Sout), reads=[('tf', 0), ('tf', 1)])
                    if c > 0:
                        S.op('dve', TS(S32, Sout, keep[:, 0:1], None, ALU.mult), reads=[('tf', 0), ('tf', 1), 'keep'], writes=['S32'])
                if c > 0:
                    S.op('act', ACP(Sbh[:, c - 1, :, :], S32), reads=['S32'], writes=[('Sbh', c - 1)])
            for sl in range(2):
                wv, wk = load_w(wsrc(rwin_d, 0, 16, 8192 + h * 512 + sl * 256, 256), 16, 256)
                for b in range(8):
                    bs = slice(b * 128, (b + 1) * 128)
                    pb = nbank()
                    S.op('pe', MM(psF[:, pb, 0:256], [(hT[:, kt, bs], wv[:, kt, :]) for kt in range(KT)]),
                         reads=[wk] + hkeys(b // 4), writes=[('ps', pb)])
                    q = nsq()
                    S.op('act', ACT(sqb[:, q, 0:256], psF[:, pb, 0:256], AF.Silu), reads=[('ps', pb)], writes=[('sq', q)])
                    S.op('dve', TT(gg[:, b, sl * 256:(sl + 1) * 256], sqb[:, q, 0:256], gnrow[:, sl * 256:(sl + 1) * 256], ALU.mult),
                         reads=[('sq', q), 'rstd'], writes=[('gg', b, sl)])
                    if sl == 0:
                        bwd_chunk(7 - b)
            S.dma('sp', DMA(S32, s0f_d[h].rearrange("(k p) e -> p k e", p=128)), writes=['S32'])
            SfbB = Kb[:, 0:4, :].rearrange("p a w -> p (a w)").rearrange("p (x c) -> p x c", c=512)
            Sfbs = [(Sfb, ['SfbA']), (SfbB, ['SfbB'] + [('Kb', b_) for b_ in range(4)])]
            S.op('act', ACP(Sfb, S32), reads=['S32'], writes=['SfbA'])
            pendT = None
            nxt = None
            if h < 7:
                nxt = [load_w(wsrc(rwin_d, 0, 16, 4096 + (h + 1) * 512 + sl * 256, 256), 16, 256) for sl in range(2)]
            for c in range(8):
                cs = slice(c * 128, (c + 1) * 128)
                po = nbank()
                S.op('pe', MM(psF[:, po, :], [(sTall[:, c, :], vv_[:, c, :]),
                                              (qsf[:, 0, cs], Sfbs[c % 2][0][:, 0, :]), (qsf[:, 1, cs], Sfbs[c % 2][0][:, 1, :]),
                                              (qsb[:, 0, cs], Sbh[:, c, 0, :]), (qsb[:, 1, cs], Sbh[:, c, 1, :])]),
                     reads=[('sT', c), ('vv', c, 0), ('vv', c, 1), Sfbs[c % 2][1][0], ('Sbh', c)] + [('qsf', j, c // 4) for j in range(2)] + [('qsb', j, c // 4) for j in range(2)],
                     writes=[('ps', po)])
                p0 = nbank()
                p1 = nbank()
                S.op('pe', MMS([(psF[:, p0, :], Kf[:, c, 0:128], vv_[:, c, :]), (psF[:, p1, :], Kf[:, c, 128:256], vv_[:, c, :])]),
                     reads=[('Kf', c), ('vv', c, 0), ('vv', c, 1)], writes=[('ps', p0), ('ps', p1)])
                toS = (c % 2 == 1)
                dst = Sout if toS else S32
                for j, pj in enumerate((p0, p1)):
                    S.op('dve', STT(dst[:, j, :], S32[:, j, :], gchf[:, hc], psF[:, pj, :], ALU.mult, ALU.add),
                         reads=['S32', 'gchf', ('ps', pj)], writes=[('tf', j)] if toS else ['S32'])
                if toS:
                    S.dma('sp', DMA(osf_d[c // 2, h].rearrange("(k p) e -> p k e", p=128), Sout), reads=[('tf', 0), ('tf', 1)])
                    if c < 7:
                        S.op('dve', TS(S32, Sout, keep[:, 0:1], None, ALU.mult), reads=[('tf', 0), ('tf', 1), 'keep'], writes=['S32'])
                if c < 7:
                    S.op('act', ACP(Sfbs[(c + 1) % 2][0], S32), reads=['S32'], writes=Sfbs[(c + 1) % 2][1])
                S.op('dve', lambda e, po=po: e.bn_stats(out=stt[:, 0, :], in_=psF[:, po, :]), reads=[('ps', po)], writes=['stt0'])
                S.op('dve', lambda e: e.bn_aggr(out=small[:, 56:58], in_=stt[:, 0, :]), reads=['stt0'], writes=['mv'])
                S.op('act', ACT(small[:, 58:59], small[:, 57:58], AF.Sqrt, bias=epsc[:, 0:1]), reads=['mv', 'epsc'], writes=['sd'])
                S.op('dve', STT(tmpf[:, 2, :], psF[:, po, :], small[:, 56:57], gg[:, c, :], ALU.subtract, ALU.mult),
                     reads=[('ps', po), 'mv', ('gg', c, 0), ('gg', c, 1)], writes=[('tf', 2)])
                S.op('dve', lambda e: e.reciprocal(out=small[:, 59:60], in_=small[:, 58:59]), reads=['sd'], writes=['rs'])
                q = nsq()
                S.op('act', ACT(sqb[:, q, :], tmpf[:, 2, :], AF.Identity, scale=small[:, 59:60]), reads=[('tf', 2), 'rs'], writes=[('sq', q)])
                def emitT(c=c, q=q, cs=cs):
                    par = c % 2
                    S.op('pe', TRS([(psB[:, par, j * 128:(j + 1) * 128], sqb[:, q, j * 128:(j + 1) * 128]) for j in range(4)], identb[:]),
                         reads=[('sq', q), 'identb'], writes=[('pb', par)])
                    S.op('act', ACP(y2T[:, 0:4, cs], psB[:, par, 0:512].rearrange("p (k t) -> p k t", t=128)),
                         reads=[('pb', par)] + [('sT', c_) for c_ in range(8)], writes=[('y2T', c // 4)])
                if pendT is not None:
                    pendT()
                pendT = emitT
                if nxt is not None and c >= 1:
                    for sl in range(2):
                        v_unit(nxt[sl][0], nxt[sl][1], sl, c - 1, 'act')
            pendT()
            if nxt is not None:
                for sl in range(2):
                    v_unit(nxt[sl][0], nxt[sl][1], sl, 7, 'act')
            slabs = [(wsrc(rwout_d, h * 512, 4, sl * 1024, 1024), 4, 1024, sl * 8) for sl in range(2)]
            resid_gemm(slabs, lambda k, gs: y2T[:, k, gs], lambda g: [('y2T', g)], modT[:, 1, 32:48], ('mod', 1))
        S.barrier()
        phase_end('l1')

        ffn(1)
        S.barrier()
        phase_end('ffn1')

        yst = carve(0, [2, D], F32)
        fT = carve(16384, [KT, 512], F32)
        for g in range(2):
            gs = slice(g * 512, (g + 1) * 512)
            pb = nbank()
            for kt in range(KT):
                q = nsq()
                S.op('act', ACT(sqb[:, q, :], xT[:, kt, gs], AF.Square), reads=[('xT', kt, g)], writes=[('sq', q)])
                S.op('pe', MM1(psF[:, pb, :], onesb[:], sqb[:, q, :], kt == 0, kt == KT - 1),
                     reads=[('sq', q), 'onesb'], writes=[('ps', pb)])
            S.op('act', ACT(rstd[:], psF[:, pb, :], AF.Sqrt, scale=1.0 / D, bias=epsc[:, 0:1]), reads=[('ps', pb), 'epsc'], writes=['rstd'])
            S.op('dve', lambda e: e.reciprocal(out=rstd[:], in_=rstd[:]), reads=['rstd'], writes=['rstd'])
            for kt in range(KT):
                S.op('dve', STT(fT[:, kt, :], xT[:, kt, gs], fing[:, kt:kt + 1], rstd[:], ALU.mult, ALU.mult),
                     reads=[('xT', kt, g), 'rstd', 'fing'], writes=[('fT', kt)])
            for bb in range(4):
                b = g * 4 + bb
                for k4 in range(4):
                    pb2 = nbank()
                    S.op('pe', TRS([(psF[:, pb2, j * 128:(j + 1) * 128], fT[:, k4 * 4 + j, bb * 128:(bb + 1) * 128]) for j in range(4)], identf[:]),
                         reads=[('fT', k4 * 4 + j) for j in range(4)] + ['identf'], writes=[('ps', pb2)])
                    if k4 % 2:
                        S.op('dve', CP(yst[:, b % 2, k4 * 512:(k4 + 1) * 512], psF[:, pb2, :]), reads=[('ps', pb2)], writes=[('yst', b % 2, k4)])
                    else:
                        S.op('act', ACP(yst[:, b % 2, k4 * 512:(k4 + 1) * 512], psF[:, pb2, :]), reads=[('ps', pb2)], writes=[('yst', b % 2, k4)])
                S.dma('sp', DMA(y_d[b * 128:(b + 1) * 128, :], yst[:, b % 2, :]), reads=[('yst', b % 2, k4) for k4 in range(4)])
    except _Stop:
        pass
    S.finish()
    with nc.Block() as block:
        S.replay(block)
    es.close()
    return nc


_NC_CACHE = {}


def _consts():
    ident = np.eye(128, dtype=np.float32)
    d = np.arange(128) % 64
    a = d // 32
    b = (d // 16) % 2
    f = d % 16
    swap = np.arange(128) + np.where(b == 0, 16, -16)
    perm = np.zeros((128, 128), np.float32)
    perm[swap, np.arange(128)] = 1.0
    t = np.arange(T)
    t_row = (t // 64).astype(np.float32)
    t_col = (t % 64).astype(np.float32)
    inv = (np.float32(10000.0) ** (-(np.arange(16, dtype=np.float32)) / np.float32(16))).astype(np.float32)
    ang = np.where(a[:, None] == 0, t_row[None, :], t_col[None, :]).astype(np.float32) * inv[f][:, None]
    ang = ang.astype(np.float32)
    C_s = np.cos(ang).astype(np.float32)
    S_s = (np.where(b[:, None] == 0, -1.0, 1.0) * np.sin(ang)).astype(np.float32)
    C_p = np.ones((128, T), np.float32)
    S_p = np.zeros((128, T), np.float32)
    qq = np.arange(128)[:, None]
    kk = np.arange(128)[None, :]
    m_s = np.zeros((128, 2, 896), np.float32)
    for par in range(2):
        m_s[:, par, 0:128] = np.where(kk >= qq, 0.0, NEG)
        m_s[:, par, 256:384] = np.where(kk <= qq, 0.0, NEG)
    m_p = np.zeros((128, 2, 896), np.float32)
    m_p[:, :, 384:896] = NEG
    m_p[:, 0, 0:128] = NEG
    m_p[:, 1, 256:384] = NEG
    jj = np.arange(128)[:, None].astype(np.float32)
    ii = np.arange(128)[None, :].astype(np.float32)
    retc = np.zeros((128, 4, 128), np.float32)
    retc[:, 0] = np.maximum(ii - jj, 0)
    retc[:, 1] = np.maximum(jj - ii, 0)
    retc[:, 2] = (ii >= jj) / 16.0
    retc[:, 3] = (jj >= ii) / 16.0
    iot = np.zeros((128, 2, 512), np.float32)
    iot[:, 0, :] = np.tile(np.arange(128, dtype=np.float32) + 1, 4)[None, :]
    iot[:, 1, :] = np.tile(128 - np.arange(128, dtype=np.float32), 4)[None, :]
    cols = np.zeros((128, 4), np.float32)
    cols[:, 0] = 127 - np.arange(128)
    cols[:, 1] = np.arange(128)
    cols[:, 2] = 128.0
    cols[:, 3] = 1.0
    return dict(ident=ident, perm=perm, C_s=C_s, S_s=S_s, C_p=C_p, S_p=S_p, m_s=m_s, m_p=m_p, retc=retc, iot=iot, cols=cols)


def _colT(v):
    v = np.asarray(v, np.float32)
    return np.ascontiguousarray(np.moveaxis(v.reshape(v.shape[:-1] + (-1, 128)), -1, 0))


def kernel(x_prompt, x_sample, cache_attn_k, cache_attn_v, state_ret_fwd, state_ret_bwd,
           c, c_ctx, ada_w, ada_b, norm_mix_g, norm_ffn_g, even_w_in, even_w_out,
           attn_sink, sgu_ln_g, sgu_ln_b, sgu_w, sgu_b, ret_w_in, ret_w_out, ret_gn_g,
           ret_decay_fwd, ret_decay_bwd, ffn_w_gate, ffn_w_up, ffn_w_down, final_g):
    f32 = lambda a: np.ascontiguousarray(np.asarray(a, dtype=np.float32))
    if 'nc' not in _NC_CACHE:
        _NC_CACHE['nc'] = build_program()
    nc = _NC_CACHE['nc']
    K = _consts()
    rep = lambda v: np.ascontiguousarray(np.broadcast_to(f32(v).reshape(1, -1), (128, f32(v).size)))
    shared = {
        "ident": K['ident'], "perm": K['perm'], "retc": K['retc'], "iotarep": K['iot'], "colsc": K['cols'],
        "ada_w": f32(ada_w),
        "ada_bT": _colT(ada_b),
        "nmgT": _colT(norm_mix_g), "nfgT": _colT(norm_ffn_g), "fingT": _colT(final_g),
        "even_w_in": f32(even_w_in[0]), "even_w_out": f32(even_w_out[0]),
        "sinkR": rep(attn_sink[0]), "lngR": rep(sgu_ln_g[0]), "lnbR": rep(sgu_ln_b[0]),
        "sguwT": np.ascontiguousarray(np.transpose(f32(sgu_w[0]), (2, 0, 1))),
        "sgubT": np.ascontiguousarray(f32(sgu_b[0]).T),
        "ret_w_in": f32(ret_w_in[0]), "ret_w_out": f32(ret_w_out[0]),
        "gngR": rep(ret_gn_g[0]), "decfR": rep(ret_decay_fwd[0]), "decbR": rep(ret_decay_bwd[0]),
        "ffn_w_gate": f32(ffn_w_gate), "ffn_w_up": f32(ffn_w_up), "ffn_w_down": f32(ffn_w_down),
    }
    xp = f32(x_prompt)
    xs = f32(x_sample)
    zkv = np.zeros((512, 256), np.float32)
    zst = np.zeros((8, 256, 512), np.float32)
    in_maps = []
    for core in range(8):
        m = dict(shared)
        if core < 4:
            m["x"] = np.ascontiguousarray(xp[core * 4:(core + 1) * 4].reshape(T, D))
            m["cvec"] = _colT(c_ctx)
            m["kctx"] = zkv
            m["vctx"] = zkv
            m["s0f"] = zst
            m["s0b"] = zst
            m["keep"] = np.zeros((128, 1), np.float32)
            m["amask"] = K['m_p']
            m["ropeC"] = K['C_p']
            m["ropeS"] = K['S_p']
        else:
            bi = core - 4
            m["x"] = np.ascontiguousarray(xs[bi])
            m["cvec"] = _colT(f32(c)[bi])
            m["kctx"] = np.ascontiguousarray(f32(cache_attn_k)[bi, 0].reshape(512, 256))
            m["vctx"] = np.ascontiguousarray(f32(cache_attn_v)[bi, 0].reshape(512, 256))
            m["s0f"] = np.ascontiguousarray(f32(state_ret_fwd)[bi, 0])
            m["s0b"] = np.ascontiguousarray(f32(state_ret_bwd)[bi, 0])
            m["keep"] = np.ones((128, 1), np.float32)
            m["amask"] = K['m_s']
            m["ropeC"] = K['C_s']
            m["ropeS"] = K['S_s']
        in_maps.append(m)
    res = run_bass_kernel_spmd(nc, in_maps, core_ids=list(range(8)))
    R = res.results
    y_prompt = np.concatenate([R[cidx]["y"].reshape(4, 256, D) for cidx in range(4)], axis=0).astype(np.float32)
    y_sample = np.stack([R[4 + b_]["y"] for b_ in range(4)], axis=0).astype(np.float32)
    okv = np.concatenate([R[cidx]["okv"].reshape(4, 256, 512) for cidx in range(4)], axis=0)
    new_k = np.ascontiguousarray(okv[:, :, 0:256]).reshape(16, 1, 256, 4, 64).astype(np.float32)
    new_v = np.ascontiguousarray(okv[:, :, 256:512]).reshape(16, 1, 256, 4, 64).astype(np.float32)
    new_sf = np.concatenate([R[cidx]["osf"] for cidx in range(4)], axis=0).reshape(16, 1, 8, 256, 512).astype(np.float32)
    new_sb = np.concatenate([R[cidx]["osb"] for cidx in range(4)], axis=0).reshape(16, 1, 8, 256, 512).astype(np.float32)
    return (y_prompt, y_sample, new_k, new_v, new_sf, new_sb)
```

```python
import numpy as np
from contextlib import ExitStack
import concourse.bass as bass
import concourse.mybir as mybir
from concourse.bass_utils import run_bass_kernel_spmd

F32 = mybir.dt.float32
BF16 = mybir.dt.bfloat16
AF = mybir.ActivationFunctionType
ALU = mybir.AluOpType

T = 1024
D = 2048
KT = 16
DFF = 5632
EPS = 1e-6
NEG = -30000.0


class Sched:
    COMPUTE = ('pe', 'dve', 'act')
    DMAQ = ('sp', 'pool')

    def __init__(self, nc, es, n_dma_sems=8):
        self.nc = nc
        self.streams = {e: [] for e in self.COMPUTE + self.DMAQ}
        self.esem = {e: es.enter_context(nc.semaphore("sem_" + e)) for e in self.COMPUTE}
        self.ecnt = {e: 0 for e in self.COMPUTE}
        self.dsem = {q: [es.enter_context(nc.semaphore("dsem_%s_%d" % (q, i))) for i in range(n_dma_sems)]
                     for q in self.DMAQ}
        self.dcnt = {q: [0] * n_dma_sems for q in self.DMAQ}
        self.drr = {q: 0 for q in self.DMAQ}
        self.waited = {e: {} for e in self.streams}
        self.last_w = {}
        self.readers = {}

    def _wait(self, eng, sem, val):
        sid = id(sem)
        if self.waited[eng].get(sid, 0) >= val:
            return
        self.waited[eng][sid] = val
        self.streams[eng].append(('w', sem, val))

    def _deps(self, eng, reads, writes):
        toks = []
        for k in reads:
            w = self.last_w.get(k)
            if w is not None:
                toks.append(w)
            if isinstance(k, tuple) and k[0] in ('ps', 'pb'):
                toks.extend(t for t in self.readers.get(k, ()) if t[2] != eng)
        for k in writes:
            w = self.last_w.get(k)
            if w is not None:
                toks.append(w)
            toks.extend(self.readers.get(k, ()))
        for t in toks:
            if t[2] == eng and eng == 'pe':
                continue
            self._wait(eng, t[0], t[1])

    def _commit(self, tok, reads, writes):
        for k in writes:
            self.last_w[k] = tok
            self.readers[k] = []
        for k in reads:
            if k in writes:
                continue
            self.readers.setdefault(k, []).append(tok)

    def op(self, eng, fn, reads=(), writes=()):
        self._deps(eng, reads, writes)
        self.ecnt[eng] += 1
        tok = (self.esem[eng], self.ecnt[eng], eng)
        self.streams[eng].append(('o', fn, self.esem[eng], 1))
        self._commit(tok, reads, writes)

    def dma(self, q, fn, reads=(), writes=()):
        self._deps(q, reads, writes)
        j = self.drr[q]
        self.drr[q] = (j + 1) % len(self.dsem[q])
        sem = self.dsem[q][j]
        if self.dcnt[q][j] > 0:
            self._wait(q, sem, 16 * self.dcnt[q][j])
        self.dcnt[q][j] += 1
        tok = (sem, 16 * self.dcnt[q][j], q)
        self.streams[q].append(('o', fn, sem, 16))
        self._commit(tok, reads, writes)

    def barrier(self, engines=('pe', 'dve', 'act', 'sp')):
        for eng in engines:
            for e in self.COMPUTE:
                if self.ecnt[e] > 0 and e != eng:
                    self._wait(eng, self.esem[e], self.ecnt[e])
            for q in self.DMAQ:
                if q == 'pool':
                    continue
                for j, sem in enumerate(self.dsem[q]):
                    if self.dcnt[q][j] > 0:
                        self._wait(eng, sem, 16 * self.dcnt[q][j])

    def finish(self):
        for q in self.DMAQ:
            for j, sem in enumerate(self.dsem[q]):
                if self.dcnt[q][j] > 0:
                    self._wait('sp', sem, 16 * self.dcnt[q][j])
        for e in self.COMPUTE:
            if self.ecnt[e] > 0:
                self._wait('sp', self.esem[e], self.ecnt[e])

    def replay(self, block):
        def run(name):
            def f(e):
                for it in self.streams[name]:
                    if it[0] == 'w':
                        e.wait_ge(it[1], it[2])
                    else:
                        ins = it[1](e)
                        ins.then_inc(it[2], it[3])
            return f
        block.tensor(run('pe'))
        block.vector(run('dve'))
        block.scalar(run('act'))
        block.gpsimd(run('pool'))
        block.sync(run('sp'))


def ACT(out, in_, func, **kw):
    return lambda e: e.activation(out=out, in_=in_, func=func, **kw)


def TT(out, in0, in1, op):
    return lambda e: e.tensor_tensor(out=out, in0=in0, in1=in1, op=op)


def TS(out, in0, s1, s2, op0, op1=None):
    if op1 is None:
        return lambda e: e.tensor_scalar(out=out, in0=in0, scalar1=s1, scalar2=None, op0=op0)
    return lambda e: e.tensor_scalar(out=out, in0=in0, scalar1=s1, scalar2=s2, op0=op0, op1=op1)


def STT(out, in0, scalar, in1, op0, op1):
    return lambda e: e.scalar_tensor_tensor(out=out, in0=in0, scalar=scalar, in1=in1, op0=op0, op1=op1)


def CP(out, in_):
    return lambda e: e.tensor_copy(out=out, in_=in_)


def ACP(out, in_):
    return lambda e: e.activation(out=out, in_=in_, func=AF.Identity)


def DMA(out, in_):
    return lambda e: e.dma_start(out=out, in_=in_)


def MM(out, pairs):
    def fn(e):
        n = len(pairs)
        ins = None
        for i, (l, r) in enumerate(pairs):
            ins = e.matmul(out, lhsT=l, rhs=r, start=(i == 0), stop=(i == n - 1))
        return ins
    return fn


def MM1(out, lhsT, rhs, start, stop):
    return lambda e: e.matmul(out, lhsT=lhsT, rhs=rhs, start=start, stop=stop)


def MMS(items):
    def fn(e):
        ins = None
        for (o, l, r) in items:
            ins = e.matmul(o, lhsT=l, rhs=r, start=True, stop=True)
        return ins
    return fn


def TRS(items, ident):
    def fn(e):
        ins = None
        for (o, i) in items:
            ins = e.transpose(o, i, ident)
        return ins
    return fn


class _Stop(Exception):
    pass


def build_program(stop=None):
    nc = bass.Bass("TRN2", target_bir_lowering=False)

    def phase_end(name):
        if stop == name:
            raise _Stop()
    es = ExitStack()

    def din(n, s):
        return nc.dram_tensor(n, list(s), F32, kind="ExternalInput").ap()

    def dout(n, s):
        return nc.dram_tensor(n, list(s), F32, kind="ExternalOutput").ap()

    def sb(n, s, d=F32):
        return es.enter_context(nc.sbuf_tensor(n, list(s), d))

    x_d = din("x", [T, D])
    cvec_d = din("cvec", [128, 16])
    kctx_d = din("kctx", [512, 256])
    vctx_d = din("vctx", [512, 256])
    s0f_d = din("s0f", [8, 256, 512])
    s0b_d = din("s0b", [8, 256, 512])
    keep_d = din("keep", [128, 1])
    amask_d = din("amask", [128, 2, 896])
    ropeC_d = din("ropeC", [128, T])
    ropeS_d = din("ropeS", [128, T])
    ident_d = din("ident", [128, 128])
    perm_d = din("perm", [128, 128])
    retc_d = din("retc", [128, 4, 128])
    iota_d = din("iotarep", [128, 2, 512])
    cols_d = din("colsc", [128, 4])
    adaw_d = din("ada_w", [2, D, 6 * D])
    adab_d = din("ada_bT", [128, 2, 96])
    nmg_d = din("nmgT", [128, 2, 16])
    nfg_d = din("nfgT", [128, 2, 16])
    fing_d = din("fingT", [128, 16])
    ewin_d = din("even_w_in", [D, 3584])
    ewout_d = din("even_w_out", [D, D])
    sink_d = din("sinkR", [128, 16])
    lng_d = din("lngR", [128, 1024])
    lnb_d = din("lnbR", [128, 1024])
    wsT_d = din("sguwT", [128, 8, 128])
    bsT_d = din("sgubT", [128, 8])
    rwin_d = din("ret_w_in", [D, 12288])
    rwout_d = din("ret_w_out", [4096, D])
    gng_d = din("gngR", [128, 4096])
    decf_d = din("decfR", [128, 8])
    decb_d = din("decbR", [128, 8])
    wg_d = din("ffn_w_gate", [2, D, DFF])
    wu_d = din("ffn_w_up", [2, D, DFF])
    wd_d = din("ffn_w_down", [2, DFF, D])

    y_d = dout("y", [T, D])
    okv_d = dout("okv", [T, 512])
    osf_d = dout("osf", [4, 8, 256, 512])
    osb_d = dout("osb", [4, 8, 256, 512])

    xT = sb("xT", [128, KT, T], F32)
    hT = sb("hT", [128, KT, T], BF16)
    NSLOT = 3
    SLOT = 4096
    wbuf = sb("wbuf", [128, NSLOT, SLOT], BF16)
    sqb = sb("sqb", [128, 3, 512], BF16)
    tmpf = sb("tmpf", [128, 3, 512], F32)
    rstd = sb("rstd", [128, 512], F32)
    rstd2 = sb("rstd2", [128, 512], F32)
    identf = sb("identf", [128, 128], F32)
    identb = sb("identb", [128, 128], BF16)
    onesb = sb("onesb", [128, 128], BF16)
    permb = sb("permb", [128, 128], BF16)
    modT = sb("modT", [128, 2, 96], F32)
    adab = sb("adab", [128, 2, 96], F32)
    nmg = sb("nmg", [128, 2, 16], F32)
    nfg = sb("nfg", [128, 2, 16], F32)
    fing = sb("fing", [128, 16], F32)
    G1 = sb("G1", [128, 2, 16], F32)
    G2 = sb("G2", [128, 2, 16], F32)
    cvec = sb("cvec_s", [128, 16], F32)
    scb = sb("scb", [128, 16], BF16)
    colsc = sb("colsc_s", [128, 4], F32)
    epsc = sb("epsc", [128, 1], F32)
    keep = sb("keep_s", [128, 1], F32)
    small = sb("small", [128, 64], F32)
    stt = sb("stt", [128, 2, 6], F32)
    vcb = sb("vcb", [128, 4, 256], BF16)
    small2 = sb("small2", [128, 32], F32)
    rsm = sb("rsm", [128, 64], F32)
    SCR = 68 * 1024
    scr = sb("scr", [128, SCR // 4], F32)

    psF = es.enter_context(nc.psum_tensor("psF", [128, 6, 512], F32))
    psB = es.enter_context(nc.psum_tensor("psB", [128, 2, 1024], BF16))

    S = Sched(nc, es)

    def carve(off, shape, dt):
        n = 1
        for s_ in shape:
            n *= s_
        nbytes = n * (4 if dt == F32 else 2)
        assert off % 4 == 0 and off + nbytes <= SCR, (off, nbytes)
        v = scr[:, off // 4:(off + nbytes + 3) // 4]
        if dt != F32:
            v = v.bitcast(BF16)
        if len(shape) == 2:
            v = v.rearrange("p (a b) -> p a b", b=shape[1])
        elif len(shape) == 3:
            v = v.rearrange("p (a b c) -> p a b c", b=shape[1], c=shape[2])
        return v

    wst = {'i': 0}
    ring = {'slots': None}

    def base_ring():
        return [(wbuf[:, i_, :], ('w', i_)) for i_ in range(NSLOT)]

    def set_ring(slots):
        ring['slots'] = slots
        wst['i'] = 0

    set_ring(base_ring())

    def next_slot():
        sl_ = ring['slots'][wst['i'] % len(ring['slots'])]
        wst['i'] += 1
        return sl_

    def wsrc(W2d, r0, ktn, c0, ncols):
        return W2d[r0:r0 + ktn * 128, c0:c0 + ncols].rearrange("(k p) n -> p k n", p=128)

    def load_w(src, ktn, ncols):
        buf, key = next_slot()
        view = buf[:, 0:ktn * ncols].rearrange("p (k n) -> p k n", n=ncols)
        S.dma('pool', DMA(view, src), writes=[key])
        return view, key

    bst = {'i': 0}

    def nbank():
        b = bst['i'] % 5
        bst['i'] += 1
        return b

    rot = {'sq': 0, 'tf': 0}

    def nsq():
        q = rot['sq'] % 3
        rot['sq'] += 1
        return q

    def ntf():
        q = rot['tf'] % 3
        rot['tf'] += 1
        return q

    def hkeys(g):
        return [('hT', kt, g) for kt in range(KT)]

    S.dma('sp', DMA(identf[:], ident_d), writes=['identf'])
    S.dma('pool', DMA(identb[:], ident_d), writes=['identb'])
    S.dma('pool', DMA(permb[:], perm_d), writes=['permb'])
    S.dma('sp', DMA(cvec[:], cvec_d), writes=['cvec'])
    S.dma('sp', DMA(adab[:], adab_d), writes=['adab'])
    S.dma('sp', DMA(nmg[:], nmg_d), writes=['nmg'])
    S.dma('sp', DMA(nfg[:], nfg_d), writes=['nfg'])
    S.dma('sp', DMA(fing[:], fing_d), writes=['fing'])
    S.dma('sp', DMA(colsc[:], cols_d), writes=['colsc'])
    S.dma('sp', DMA(keep[:], keep_d), writes=['keep'])
    S.dma('pool', DMA(vcb[:], vctx_d.rearrange("(b p) c -> p b c", p=128)), writes=['vcb'])
    S.op('dve', lambda e: e.memset(onesb[:], 1.0), writes=['onesb'])
    S.op('dve', lambda e: e.memset(epsc[:], EPS), writes=['epsc'])
    S.op('act', ACT(scb[:], cvec[:], AF.Silu), reads=['cvec'], writes=['sc'])

    ADAB = 5

    def ada_slab(l, sl):
        wv, wk = load_w(wsrc(adaw_d[l], 0, 16, sl * 256, 256), 16, 256)
        for j in range(2):
            n = sl * 2 + j
            S.op('pe', MM(psF[:, ADAB, n:n + 1], [(wv[:, kt, j * 128:(j + 1) * 128], scb[:, kt:kt + 1]) for kt in range(16)]),
                 reads=[wk, 'sc'], writes=[('ps', ADAB)])

    def ada_final(l, n0, n1):
        S.op('dve', TT(modT[:, l, n0:n1], psF[:, ADAB, n0:n1], adab[:, l, n0:n1], ALU.add), reads=[('ps', ADAB), 'adab'], writes=[('mod', l)])
        if n0 <= 16 and n1 >= 32:
            S.op('dve', STT(G1[:, l, :], modT[:, l, 16:32], 1.0, nmg[:, l, :], ALU.add, ALU.mult), reads=[('mod', l), 'nmg'], writes=[('G1', l)])
        if n0 <= 64 and n1 >= 80:
            S.op('dve', STT(G2[:, l, :], modT[:, l, 64:80], 1.0, nfg[:, l, :], ALU.add, ALU.mult), reads=[('mod', l), 'nfg'], writes=[('G2', l)])

    class _Ticker:
        def __init__(self, l, slabs):
            self.slabs = [(l, sl_) for sl_ in slabs]

        def tick(self, n=1):
            for _ in range(n):
                if self.slabs:
                    it = self.slabs.pop(0)
                    if callable(it):
                        it()
                        self.tick()
                    else:
                        ada_slab(it[0], it[1])

        def flush(self):
            self.tick(len(self.slabs))

    def norm_to_hT(Gc, Bc, gkeys):
        rbuf = [(rstd, 'rstd'), (rstd2, 'rstd2')]
        for g in range(2):
            gs = slice(g * 512, (g + 1) * 512)
            rb, rkey = rbuf[g]
            pb = nbank()
            for kt in range(KT):
                q = nsq()
                if kt % 2 == 0:
                    S.op('act', ACT(sqb[:, q, :], xT[:, kt, gs], AF.Square), reads=[('xT', kt, g)], writes=[('sq', q)])
                else:
                    S.op('dve', TT(sqb[:, q, :], xT[:, kt, gs], xT[:, kt, gs], ALU.mult), reads=[('xT', kt, g)], writes=[('sq', q)])
                S.op('pe', MM1(psF[:, pb, :], onesb[:], sqb[:, q, :], kt == 0, kt == KT - 1),
                     reads=[('sq', q), 'onesb'], writes=[('ps', pb)])
            S.op('act', ACT(rb[:], psF[:, pb, :], AF.Sqrt, scale=1.0 / D, bias=epsc[:, 0:1]), reads=[('ps', pb), 'epsc'], writes=[rkey])
            S.op('dve', (lambda rb: (lambda e: e.reciprocal(out=rb[:], in_=rb[:])))(rb), reads=[rkey], writes=[rkey])
        for g in range(2):
            gs = slice(g * 512, (g + 1) * 512)
            rb, rkey = rbuf[g]
            for kt in range(KT):
                q = ntf()
                S.op('dve', TT(tmpf[:, q, :], xT[:, kt, gs], rb[:], ALU.mult), reads=[('xT', kt, g), rkey], writes=[('tf', q)])
                S.op('act', ACT(hT[:, kt, gs], tmpf[:, q, :], AF.Identity, scale=Gc[:, kt:kt + 1], bias=Bc[:, kt:kt + 1]),
                     reads=[('tf', q)] + gkeys, writes=[('hT', kt, g)])

    def resid_gemm(slabs, rhs_fn, rhs_keys_fn, gate, gkey, ticker=None):
        for (src, ktn, ncols, n0) in slabs:
            wv, wk = load_w(src, ktn, ncols)
            for j in range(ncols // 128):
                n = n0 + j
                for g in range(2):
                    gs = slice(g * 512, (g + 1) * 512)
                    pb = nbank()
                    S.op('pe', MM(psF[:, pb, :], [(wv[:, k, j * 128:(j + 1) * 128], rhs_fn(k, gs)) for k in range(ktn)]),
                         reads=[wk] + rhs_keys_fn(g), writes=[('ps', pb)])
                    S.op('dve', STT(xT[:, n, gs], psF[:, pb, :], gate[:, n:n + 1], xT[:, n, gs], ALU.mult, ALU.add),
                         reads=[('ps', pb), gkey, ('xT', n, g)], writes=[('xT', n, g)])
            if ticker is not None:
                ticker.tick()

    def ffn(l, ticker=None):
        S.barrier(engines=('pool',))
        set_ring(base_ring() + [(carve(45056 + i_ * 8192, [4096], BF16), ('wx', i_)) for i_ in range(3)])
        norm_to_hT(G2[:, l, :], modT[:, l, 48:64], [('G2', l), ('mod', l)])
        actT = carve(0, [22, T], BF16)

        for half in range(2):
            f0 = half * 22
            for sp_ in range(11):
                c0 = (f0 + sp_ * 2) * 128
                wvg, wkg = load_w(wsrc(wg_d[l], 0, 16, c0, 256), 16, 256)
                wvu, wku = load_w(wsrc(wu_d[l], 0, 16, c0, 256), 16, 256)
                for j in range(2):
                    fl = sp_ * 2 + j
                    for g in range(2):
                        gs = slice(g * 512, (g + 1) * 512)
                        pg = nbank()
                        S.op('pe', MM(psF[:, pg, :], [(wvg[:, kt, j * 128:(j + 1) * 128], hT[:, kt, gs]) for kt in range(KT)]),
                             reads=[wkg] + hkeys(g), writes=[('ps', pg)])
                        pu = nbank()
                        S.op('pe', MM(psF[:, pu, :], [(wvu[:, kt, j * 128:(j + 1) * 128], hT[:, kt, gs]) for kt in range(KT)]),
                             reads=[wku] + hkeys(g), writes=[('ps', pu)])
                        q = nsq()
                        S.op('act', ACT(sqb[:, q, :], psF[:, pg, :], AF.Silu), reads=[('ps', pg)], writes=[('sq', q)])
                        S.op('dve', TT(actT[:, fl, gs], psF[:, pu, :], sqb[:, q, :], ALU.mult),
                             reads=[('ps', pu), ('sq', q)], writes=[('actT', fl, g)])
                if ticker is not None:
                    ticker.tick()
            slabs = [(wsrc(wd_d[l], f0 * 128, 22, n * 128, 128), 22, 128, n) for n in range(16)]
            resid_gemm(slabs, lambda k, gs: actT[:, k, gs], lambda g: [('actT', k, g) for k in range(22)],
                       modT[:, l, 80:96], ('mod', l), ticker=ticker)
        if ticker is not None:
            ticker.flush()
        set_ring(base_ring())

    try:
        xin = carve(0, [2, D], F32)
        set_ring(base_ring() + [(carve(45056 + i_ * 8192, [4096], BF16), ('wx', i_)) for i_ in range(3)])
        for b in range(8):
            S.dma('sp', DMA(xin[:, b % 2, :], x_d[b * 128:(b + 1) * 128, :]), writes=[('xin', b % 2)])
            for k4 in range(4):
                pb = nbank()
                S.op('pe', TRS([(psF[:, pb, j * 128:(j + 1) * 128], xin[:, b % 2, (k4 * 4 + j) * 128:(k4 * 4 + j + 1) * 128]) for j in range(4)], identf[:]),
                     reads=[('xin', b % 2), 'identf'], writes=[('ps', pb)])
                S.op('dve' if k4 % 2 else 'act',
                     (CP if k4 % 2 else ACP)(xT[:, k4 * 4:(k4 + 1) * 4, b * 128:(b + 1) * 128], psF[:, pb, :].rearrange("p (k t) -> p k t", t=128)),
                     reads=[('ps', pb)], writes=[('xT', k4 * 4 + j, b // 4) for j in range(4)])
            ada_slab(0, 2 * b)
            ada_slab(0, 2 * b + 1)
        set_ring(base_ring())
        lfT = rsm[:, 0:8]
        lbT = rsm[:, 8:16]
        koutf = rsm[:, 16:24]
        koutb = rsm[:, 24:32]
        gchf = rsm[:, 32:40]
        gchb = rsm[:, 40:48]
        tmp8 = rsm[:, 48:56]
        decf = rsm[:, 56:64]
        S.dma('sp', DMA(decf, decf_d), writes=['decf'])
        S.op('act', ACT(tmp8, decf, AF.Exp, scale=-1.0), reads=['decf'], writes=['tmp8'])
        S.op('act', ACT(tmp8, tmp8, AF.Ln, bias=colsc[:, 3:4]), reads=['tmp8', 'colsc'], writes=['tmp8'])
        S.op('dve', TS(lfT, tmp8, -1.0, None, ALU.mult), reads=['tmp8'], writes=['lfT'])
        S.dma('sp', DMA(decf, decb_d), reads=['tmp8'], writes=['decf'])
        S.op('act', ACT(tmp8, decf, AF.Exp, scale=-1.0), reads=['decf', 'lfT'], writes=['tmp8'])
        S.op('act', ACT(tmp8, tmp8, AF.Ln, bias=colsc[:, 3:4]), reads=['tmp8', 'colsc'], writes=['tmp8'])
        S.op('dve', TS(lbT, tmp8, -1.0, None, ALU.mult), reads=['tmp8'], writes=['lbT'])
        for (dst, src, col, scl, key) in ((koutf, lfT, 0, 1.0 / 16, 'koutf'), (koutb, lbT, 1, 1.0 / 16, 'koutb'),
                                          (gchf, lfT, 2, 1.0, 'gchf'), (gchb, lbT, 2, 1.0, 'gchb')):
            S.op('dve', TS(dst, src, colsc[:, col:col + 1], None, ALU.mult), reads=['lfT', 'lbT', 'colsc'], writes=[key])
            S.op('act', ACT(dst, dst, AF.Exp), reads=[key], writes=[key])
            if scl != 1.0:
                S.op('dve', TS(dst, dst, scl, None, ALU.mult), reads=[key], writes=[key])
        ada_final(0, 0, 32)
        tk0 = _Ticker(0, range(16, 24))
        tk0b = _Ticker(0, range(24, 40))
        phase_end('ada0')
        S.barrier()
        phase_end('xin')

        norm_to_hT(G1[:, 0, :], modT[:, 0, 0:16], [('G1', 0), ('mod', 0)])
        phase_end('l0norm')

        qT = carve(0, [8, T], BF16)
        kT = carve(16384, [4, T], BF16)
        vtok = carve(24576, [8, 256], BF16)
        ubf = carve(28672, [8, 1024], BF16)
        vgbf = carve(45056, [8, 1024], BF16)
        TMP0 = 61440
        ropeC = carve(TMP0, [T], F32)
        ropeS = carve(TMP0 + 4096, [T], F32)
        S.dma('sp', DMA(ropeC, ropeC_d), writes=['ropeC'])
        S.dma('sp', DMA(ropeS, ropeS_d), writes=['ropeS'])

        def rope_evac(pb, g, dst, dkey):
            gs = slice(g * 512, (g + 1) * 512)
            q1 = ntf()
            S.op('dve', TT(tmpf[:, q1, :], psF[:, pb, :], ropeC[:, gs], ALU.mult), reads=[('ps', pb), 'ropeC'], writes=[('tf', q1)])
            qb_ = nsq()
            S.op('act', ACP(sqb[:, qb_, :], psF[:, pb, :]), reads=[('ps', pb), ('tf', q1)], writes=[('sq', qb_)])
            pb2 = nbank()
            S.op('pe', MM(psF[:, pb2, :], [(permb[:], sqb[:, qb_, :])]), reads=[('sq', qb_), 'permb'], writes=[('ps', pb2)])
            q2 = ntf()
            S.op('dve', TT(tmpf[:, q2, :], psF[:, pb2, :], ropeS[:, gs], ALU.mult), reads=[('ps', pb2), 'ropeS'], writes=[('tf', q2)])
            S.op('dve', TT(dst, tmpf[:, q1, :], tmpf[:, q2, :], ALU.add), reads=[('tf', q1), ('tf', q2)], writes=[dkey])

        for sl in range(4):
            wv, wk = load_w(wsrc(ewin_d, 0, 16, sl * 256, 256), 16, 256)
            for j in range(2):
                tl = sl * 2 + j
                for g in range(2):
                    gs = slice(g * 512, (g + 1) * 512)
                    pb = nbank()
                    S.op('pe', MM(psF[:, pb, :], [(wv[:, kt, j * 128:(j + 1) * 128], hT[:, kt, gs]) for kt in range(KT)]),
                         reads=[wk] + hkeys(g), writes=[('ps', pb)])
                    rope_evac(pb, g, qT[:, tl, gs], ('qT', tl, g))
            tk0.tick(1)
        phase_end('l0q')
        for sl in range(2):
            buf_, wk = next_slot()
            wv = buf_[:, 0:16 * 256].rearrange("p (k n) -> p k n", n=256)
            for j in range(2):
                kvh = sl * 2 + j
                for rep in range(2):
                    S.dma('pool', DMA(wv[:, :, j * 128 + rep * 64:j * 128 + rep * 64 + 64], wsrc(ewin_d, 0, 16, 1024 + kvh * 64, 64)), writes=[wk])
            for j in range(2):
                kvh = sl * 2 + j
                for g in range(2):
                    gs = slice(g * 512, (g + 1) * 512)
                    pb = nbank()
                    S.op('pe', MM(psF[:, pb, :], [(wv[:, kt, j * 128:(j + 1) * 128], hT[:, kt, gs]) for kt in range(KT)]),
                         reads=[wk] + hkeys(g), writes=[('ps', pb)])
                    rope_evac(pb, g, kT[:, kvh, gs], ('kT', kvh, g))
            tk0.tick(1)

        phase_end('l0projB')
        okst = rstd[:].rearrange("p (a b) -> p a b", b=256)
        okc = {'i': 0}
        for sl in range(10):
            c0 = 1024 + sl * 256
            wv, wk = load_w(wsrc(ewin_d, 0, 16, c0, 256), 16, 256)
            for b in range(8):
                bs = slice(b * 128, (b + 1) * 128)
                pb = nbank()
                S.op('pe', MM(psF[:, pb, 0:256], [(hT[:, kt, bs], wv[:, kt, :]) for kt in range(KT)]),
                     reads=[wk] + hkeys(b // 4), writes=[('ps', pb)])
                if sl < 2:
                    q = okc['i'] % 2
                    okc['i'] += 1
                    S.op('act', ACP(okst[:, q, :], psF[:, pb, 0:256]), reads=[('ps', pb)], writes=[('okst', q), 'rstd'])
                    S.dma('sp', DMA(okv_d[bs, sl * 256:(sl + 1) * 256], okst[:, q, :]), reads=[('okst', q)])
                    if sl == 1:
                        S.op('dve', CP(vtok[:, b, :], psF[:, pb, 0:256]), reads=[('ps', pb)], writes=[('vtok', b)])
                elif sl < 6:
                    cc = (sl - 2) * 256
                    S.op('act', ACT(ubf[:, b, cc:cc + 256], psF[:, pb, 0:256], AF.Gelu_apprx_tanh), reads=[('ps', pb)], writes=[('ubf', b, sl)])
                else:
                    cc = (sl - 6) * 256
                    S.op('act', ACT(vgbf[:, b, cc:cc + 256], psF[:, pb, 0:256], AF.Gelu_apprx_tanh), reads=[('ps', pb)], writes=[('vgbf', b, sl)])
            tk0.tick(1)
        tk0.flush()
        ada_final(0, 32, 48)
        S.barrier()
        phase_end('l0proj')

        lng = carve(TMP0, [1024], F32)
        lnb = carve(TMP0 + 4096, [1024], F32)
        S.dma('sp', DMA(lng, lng_d), writes=['lng'])
        S.dma('sp', DMA(lnb, lnb_d), writes=['lnb'])
        wsT = rstd[:].bitcast(BF16).rearrange("p (g q) -> p g q", q=128)
        for hf in range(2):
            S.dma('sp', DMA(tmpf[:, 2, :].rearrange("p (g q) -> p g q", q=128), wsT_d[:, hf * 4:(hf + 1) * 4, :]), writes=[('tf', 2)])
            S.op('act', ACP(wsT[:, hf * 4:(hf + 1) * 4, :], tmpf[:, 2, :].rearrange("p (g q) -> p g q", q=128)), reads=[('tf', 2)], writes=['wsT', 'rstd'])
        bsT = small[:, 0:8]
        S.dma('sp', DMA(bsT, bsT_d), writes=['bsT'])
        vn32 = tmpf[:, 0:2, :].rearrange("p a b -> p (a b)")
        vnb = sqb[:, 0:2, :].rearrange("p a b -> p (a b)")
        mvall = small[:, 32:48].rearrange("p (b t) -> p b t", t=2)
        sdall = small[:, 48:56]
        rsall = small[:, 56:64]
        for b in range(8):
            vg_b = vgbf[:, b, :]
            S.op('dve', lambda e, vg_b=vg_b: e.bn_stats(out=stt[:, 0, :], in_=vg_b[:, 0:512]), reads=[('vgbf', b, s_) for s_ in range(6, 10)], writes=['stt0'])
            S.op('dve', lambda e, vg_b=vg_b: e.bn_stats(out=stt[:, 1, :], in_=vg_b[:, 512:1024]), reads=[('vgbf', b, s_) for s_ in range(6, 10)], writes=['stt1'])
            S.op('dve', lambda e, b=b: e.bn_aggr(out=mvall[:, b, :], in_=stt[:].rearrange("p a b -> p (a b)")), reads=['stt0', 'stt1'], writes=['mv'])
        S.op('act', ACT(sdall, mvall[:, :, 1], AF.Sqrt, bias=epsc[:, 0:1]), reads=['mv', 'epsc'], writes=['sd'])
        S.op('dve', lambda e: e.reciprocal(out=rsall, in_=sdall), reads=['sd'], writes=['rs'])
        for b in range(8):
            bs = slice(b * 128, (b + 1) * 128)
            vg_b = vgbf[:, b, :]
            S.op('dve', TS(vn32, vg_b, mvall[:, b, 0:1], rsall[:, b:b + 1], ALU.subtract, ALU.mult), reads=['mv', 'rs', ('vgbf', b, 6)], writes=[('tf', 0), ('tf', 1)])
            S.op('dve', TT(vn32, vn32, lng, ALU.mult), reads=[('tf', 0), 'lng'], writes=[('tf', 0), ('tf', 1)])
            S.op('dve', TT(vnb, vn32, lnb, ALU.add), reads=[('tf', 0), 'lnb'], writes=[('sq', 0), ('sq', 1)])
            S.op('pe', MMS([(psF[:, g_ // 4, (g_ % 4) * 128:(g_ % 4 + 1) * 128], wsT[:, g_, :], vnb[:, g_ * 128:(g_ + 1) * 128]) for g_ in range(8)]),
                 reads=[('sq', 0), ('sq', 1), 'wsT'], writes=[('ps', 0), ('ps', 1)])
            for g_ in range(8):
                S.op('dve', STT(vg_b[:, g_ * 128:(g_ + 1) * 128], psF[:, g_ // 4, (g_ % 4) * 128:(g_ % 4 + 1) * 128], bsT[:, g_:g_ + 1],
                                ubf[:, b, g_ * 128:(g_ + 1) * 128], ALU.add, ALU.mult),
                     reads=[('ps', g_ // 4), 'bsT'] + [('ubf', b, s_) for s_ in range(2, 6)], writes=[('gout', b)])
            par = b % 2
            S.op('pe', TRS([(psB[:, par, j * 128:(j + 1) * 128], vg_b[:, j * 128:(j + 1) * 128]) for j in range(8)], identb[:]),
                 reads=[('gout', b), 'identb'], writes=[('pb', par)])
            S.op('act', ACP(hT[:, 8:16, bs], psB[:, par, :].rearrange("p (k t) -> p k t", t=128)), reads=[('pb', par)],
                 writes=[('hT', 8 + j, b // 4) for j in range(8)])
            tk0b.tick(2)
        tk0b.flush()
        ada_final(0, 48, 80)
        S.barrier()
        phase_end('sgu')

        A0 = 28672
        kcT = carve(A0, [4, 512], BF16)
        amask = carve(A0 + 4096, [2, 896], BF16)
        Pb = carve(A0 + 8192, [2, 896], BF16)
        PTb = carve(A0 + 12288, [2, 896], BF16)
        atok = carve(A0 + 16384, [2, 1024], BF16)
        kcd = carve(A0 + 20480, [2, 128], BF16)
        sinkT = small[:, 16:32]
        am32 = carve(A0 + 25600, [2, 896], F32)
        kc32 = carve(A0 + 21504, [4, 256], F32)
        S.dma('sp', DMA(am32, amask_d), writes=['am32'])
        S.op('dve', CP(amask, am32), reads=['am32'], writes=['amask'])
        S.dma('sp', DMA(kc32, kctx_d.rearrange("(b p) c -> p b c", p=128)), writes=['kcb'])
        S.dma('sp', DMA(sinkT, sink_d), writes=['sinkT'])
        cnt = 0
        for kvh in range(4):
            for blk in range(4):
                q = cnt % 2
                cnt += 1
                S.op('dve', CP(kcd[:, q, :].rearrange("p (r d) -> p r d", r=2),
                               kc32[:, blk, kvh * 64:(kvh + 1) * 64].unsqueeze(1).broadcast_to([128, 2, 64])),
                     reads=['kcb'], writes=[('kcd', q)])
                S.op('pe', TRS([(psB[:, q, 0:128], kcd[:, q, :])], identb[:]), reads=[('kcd', q), 'identb'], writes=[('pb', q)])
                S.op('act', ACP(kcT[:, kvh, blk * 128:(blk + 1) * 128], psB[:, q, 0:128]), reads=[('pb', q)], writes=['kcT'])

        phase_end('kct')
        negm = small[:, 32:48]
        rsum = small[:, 48:64]
        mxc = small2[:, 16:17]
        sk = small2[:, 0:16]
        for i in range(8):
            isl = slice(i * 128, (i + 1) * 128)
            lo = max(i - 1, 0)
            hi = min(i + 1, 7)
            nloc = (hi - lo + 1) * 128
            offlo = lo - (i - 1)
            par = i % 2
            c0 = 512 - nloc
            W_ = nloc + 512
            nblk = W_ // 128

            def s_op(h, isl=isl, lo=lo, hi=hi, nloc=nloc, offlo=offlo, par=par, c0=c0, i=i):
                sp_ = h % 2
                hb = (h % 2) * 64
                tl = h // 2
                kvh = h // 4
                Sv = psF[:, 2 * sp_:2 * sp_ + 2, :].rearrange("p a b -> p (a b)")
                o1 = Sv[:, c0:512]
                o2 = Sv[:, 512:1024]
                l1 = qT[hb:hb + 64, tl, isl]
                r1 = kT[hb:hb + 64, kvh, lo * 128:(hi + 1) * 128]
                r2 = amask[:, par, offlo * 128:offlo * 128 + nloc]
                r3 = kcT[hb:hb + 64, kvh, :]
                r4 = amask[:, par, 384:896]
                idb = identb[:]

                def fn(e):
                    e.matmul(o1, lhsT=l1, rhs=r1, start=True, stop=False)
                    e.matmul(o1, lhsT=idb, rhs=r2, start=False, stop=True)
                    e.matmul(o2, lhsT=l1, rhs=r3, start=True, stop=False)
                    return e.matmul(o2, lhsT=idb, rhs=r4, start=False, stop=True)
                S.op('pe', fn, reads=[('qT', tl, i // 4), ('kT', kvh, 0), ('kT', kvh, 1), 'kcT', 'amask', 'identb'],
                     writes=[('ps', 2 * sp_), ('ps', 2 * sp_ + 1)])

            S.op('dve', lambda e: e.memset(rsum, 0.0), writes=[('rsum', h_) for h_ in range(16)])
            def emit_max(h, c0=c0):
                sp_ = h % 2
                Sv = psF[:, 2 * sp_:2 * sp_ + 2, :].rearrange("p a b -> p (a b)")
                S.op('dve', lambda e, Sv=Sv, c0=c0: e.reduce_max(out=mxc, in_=Sv[:, c0:1024], axis=mybir.AxisListType.X),
                     reads=[('ps', 2 * sp_), ('ps', 2 * sp_ + 1)], writes=['mxc'])
                S.op('dve', TS(negm[:, h:h + 1], mxc, -0.125, None, ALU.mult), reads=['mxc'], writes=[('negm', h)])

            def emit_exp(h, c0=c0, W_=W_):
                sp_ = h % 2
                Sv = psF[:, 2 * sp_:2 * sp_ + 2, :].rearrange("p a b -> p (a b)")
                S.op('act', ACT(Pb[:, sp_, 0:W_], Sv[:, c0:1024], AF.Exp, scale=0.125, bias=negm[:, h:h + 1], accum_out=rsum[:, h:h + 1]),
                     reads=[('ps', 2 * sp_), ('ps', 2 * sp_ + 1), ('negm', h)], writes=[('P', sp_), ('rsum', h)])

            def emit_T(h, nblk=nblk):
                sp_ = h % 2
                S.op('pe', TRS([(psB[:, sp_, kb * 128:(kb + 1) * 128], Pb[:, sp_, kb * 128:(kb + 1) * 128]) for kb in range(nblk)], identb[:]),
                     reads=[('P', sp_), 'identb'], writes=[('pb', sp_)])

            def emit_evac(h, W_=W_):
                sp_ = h % 2
                S.op('dve', CP(PTb[:, sp_, 0:W_], psB[:, sp_, 0:W_]), reads=[('pb', sp_)], writes=[('PT', sp_)])

            def emit_PV(h, nblk=nblk, lo=lo, hi=hi):
                sp_ = h % 2
                kvh = h // 4
                Ov = psF[:, 4:6, :].rearrange("p a b -> p (a b)")
                pairs = []
                for kb in range(nblk):
                    if kb < nblk - 4:
                        vv = vtok[:, lo + kb, kvh * 64:(kvh + 1) * 64]
                    else:
                        vv = vcb[:, kb - (nblk - 4), kvh * 64:(kvh + 1) * 64]
                    pairs.append((PTb[:, sp_, kb * 128:(kb + 1) * 128], vv))
                S.op('pe', MM(Ov[:, h * 64:(h + 1) * 64], pairs), reads=[('PT', sp_), 'vcb'] + [('vtok', b_) for b_ in range(lo, hi + 1)],
                     writes=[('ps', 4), ('ps', 5)])

            s_op(0)
            s_op(1)
            emit_max(0)
            emit_exp(0)
            for h in range(16):
                if h + 1 < 16:
                    emit_max(h + 1)
                    emit_exp(h + 1)
                emit_T(h)
                if h + 2 < 16:
                    s_op(h + 2)
                emit_evac(h)
                emit_PV(h)
            allh = [('negm', h) for h in range(16)]
            S.op('dve', TT(sk, sinkT, negm, ALU.add), reads=allh + ['sinkT'], writes=['sk'])
            S.op('act', ACT(sk, sk, AF.Exp), reads=['sk'], writes=['sk'])
            S.op('dve', TT(sk, sk, rsum, ALU.add), reads=['sk'] + [('rsum', h) for h in range(16)], writes=['sk'])
            S.op('dve', lambda e: e.reciprocal(out=sk, in_=sk), reads=['sk'], writes=['sk'])
            Ov = psF[:, 4:6, :].rearrange("p a b -> p (a b)")
            S.op('dve', TT(atok[:, par, :].rearrange("p (h d) -> p h d", d=64), Ov.rearrange("p (h d) -> p h d", d=64),
                           sk.unsqueeze(2).broadcast_to([128, 16, 64]), ALU.mult),
                 reads=['sk', ('ps', 4), ('ps', 5)], writes=[('atok', par)])
            S.op('pe', TRS([(psB[:, par, j * 128:(j + 1) * 128], atok[:, par, j * 128:(j + 1) * 128]) for j in range(8)], identb[:]),
                 reads=[('atok', par), 'identb'], writes=[('pb', par)])
            S.op('act', ACP(hT[:, 0:8, isl], psB[:, par, :].rearrange("p (k t) -> p k t", t=128)), reads=[('pb', par)],
                 writes=[('hT', j, i // 4) for j in range(8)])

        phase_end('attn')
        slabs = [(wsrc(ewout_d, 0, 16, sl * 256, 256), 16, 256, sl * 2) for sl in range(8)]
        resid_gemm(slabs, lambda k, gs: hT[:, k, gs], hkeys, modT[:, 0, 32:48], ('mod', 0))
        S.barrier()
        phase_end('l0out')

        tk1 = _Ticker(0, range(40, 48))
        tk1.slabs.append(lambda: ada_final(0, 80, 96))
        tk1.slabs.extend((1, sl_) for sl_ in range(48))
        ffn(0, tk1)
        phase_end('ffn0')
        tk1.flush()
        ada_final(1, 0, 96)
        S.barrier()
        phase_end('ffn0ada1')

        norm_to_hT(G1[:, 1, :], modT[:, 1, 0:16], [('G1', 1), ('mod', 1)])
        rq = carve(0, [2, T], BF16)
        rk = carve(4096, [2, T], BF16)
        y2T = carve(0, [4, T], BF16)
        qsf = carve(8192, [2, T], BF16)
        qsb = carve(12288, [2, T], BF16)
        Kf = carve(16384, [8, 256], BF16)
        Kb = carve(20480, [8, 256], BF16)
        vv_ = carve(24576, [8, 512], BF16)
        gg = carve(32768, [8, 512], BF16)
        Sbh = carve(40960, [8, 2, 512], BF16)
        S32 = carve(57344, [2, 512], F32)
        Sout = tmpf[:, 0:2, :]
        Sfb = carve(61440, [2, 512], BF16)
        sTall = carve(63488, [8, 128], BF16)
        Mh = carve(65536, [128], F32)
        m2 = carve(66048, [128], F32)
        retcs = carve(66560, [4, 128], BF16)
        iotas = carve(67584, [2, 512], BF16)
        gnrow = rstd
        S.dma('sp', DMA(tmpf[:, 2, :].rearrange("p (a b) -> p a b", b=128), retc_d), writes=[('tf', 2)])
        S.op('dve', CP(retcs, tmpf[:, 2, :].rearrange("p (a b) -> p a b", b=128)), reads=[('tf', 2)], writes=['retcs'])
        S.dma('sp', DMA(tmpf[:, 0:2, :], iota_d), writes=[('tf', 0), ('tf', 1)])
        S.op('dve', CP(iotas, tmpf[:, 0:2, :]), reads=[('tf', 0), ('tf', 1)], writes=['iotas'])
        qinF = tmpf[:, 0, :]
        qinB = tmpf[:, 1, :]

        def v_unit(wv, wk, sl, b, eng):
            bs = slice(b * 128, (b + 1) * 128)
            pb = nbank()
            S.op('pe', MM(psF[:, pb, 0:256], [(hT[:, kt, bs], wv[:, kt, :]) for kt in range(KT)]),
                 reads=[wk] + hkeys(b // 4), writes=[('ps', pb)])
            if eng == 'dve':
                S.op('dve', CP(vv_[:, b, sl * 256:(sl + 1) * 256], psF[:, pb, 0:256]), reads=[('ps', pb)], writes=[('vv', b, sl)])
            else:
                S.op('act', ACP(vv_[:, b, sl * 256:(sl + 1) * 256], psF[:, pb, 0:256]), reads=[('ps', pb)], writes=[('vv', b, sl)])

        for h in range(8):
            hc = slice(h, h + 1)
            S.op('act', ACT(Mh, retcs[:, 0, :], AF.Exp, scale=lfT[:, hc]), reads=['retcs', 'lfT'], writes=['Mh'])
            S.op('dve', TT(Mh, Mh, retcs[:, 2, :], ALU.mult), reads=['Mh', 'retcs'], writes=['Mh'])
            S.op('act', ACT(m2, retcs[:, 1, :], AF.Exp, scale=lbT[:, hc]), reads=['retcs', 'lbT'], writes=['m2'])
            S.op('dve', TT(m2, m2, retcs[:, 3, :], ALU.mult), reads=['m2', 'retcs'], writes=['m2'])
            S.op('dve', TT(Mh, Mh, m2, ALU.add), reads=['Mh', 'm2'], writes=['Mh'])
            S.op('act', ACT(qinF, iotas[:, 0, :], AF.Exp, scale=lfT[:, hc]), reads=['iotas', 'lfT'], writes=[('tf', 0)])
            S.op('act', ACT(qinB, iotas[:, 1, :], AF.Exp, scale=lbT[:, hc]), reads=['iotas', 'lbT'], writes=[('tf', 1)])
            S.dma('sp', DMA(gnrow[:], gng_d[:, h * 512:(h + 1) * 512]), writes=['rstd'])
            S.dma('sp', DMA(S32, s0b_d[h].rearrange("(k p) e -> p k e", p=128)), writes=['S32'])

            wv, wk = load_w(wsrc(rwin_d, 0, 16, h * 256, 256), 16, 256)
            for j in range(2):
                for g in range(2):
                    gs = slice(g * 512, (g + 1) * 512)
                    pb = nbank()
                    S.op('pe', MM(psF[:, pb, :], [(wv[:, kt, j * 128:(j + 1) * 128], hT[:, kt, gs]) for kt in range(KT)]),
                         reads=[wk] + hkeys(g), writes=[('ps', pb)])
                    S.op('act', ACP(rq[:, j, gs], psF[:, pb, :]), reads=[('ps', pb)], writes=[('rq', j, g), ('y2T', g)])
                    S.op('dve', TT(qsf[:, j, gs], psF[:, pb, :], qinF, ALU.mult), reads=[('ps', pb), ('tf', 0)], writes=[('qsf', j, g)])
                    S.op('dve', TT(qsb[:, j, gs], psF[:, pb, :], qinB, ALU.mult), reads=[('ps', pb), ('tf', 1)], writes=[('qsb', j, g)])
            wv, wk = load_w(wsrc(rwin_d, 0, 16, 2048 + h * 256, 256), 16, 256)
            for b in range(8):
                bs = slice(b * 128, (b + 1) * 128)
                pb = nbank()
                S.op('pe', MM(psF[:, pb, 0:256], [(hT[:, kt, bs], wv[:, kt, :]) for kt in range(KT)]),
                     reads=[wk] + hkeys(b // 4), writes=[('ps', pb)])
                q = nsq()
                S.op('act', ACP(sqb[:, q, 0:256], psF[:, pb, 0:256]), reads=[('ps', pb)], writes=[('sq', q)])
                S.op('dve', TS(Kb[:, b, :], psF[:, pb, 0:256], koutb[:, hc], None, ALU.mult), reads=[('ps', pb), 'koutb'],
                     writes=[('Kb', b)] + (['SfbB'] if b < 4 else []))
                S.op('act', ACT(Kf[:, b, :], psF[:, pb, 0:256], AF.Identity, scale=koutf[:, hc]), reads=[('ps', pb), 'koutf'], writes=[('Kf', b)])
                par = b % 2
                S.op('pe', TRS([(psB[:, par, j * 128:(j + 1) * 128], sqb[:, q, j * 128:(j + 1) * 128]) for j in range(2)], identb[:]),
                     reads=[('sq', q), 'identb'], writes=[('pb', par)])
                S.op('dve' if b % 2 else 'act', (CP if b % 2 else ACP)(rk[:, 0:2, bs], psB[:, par, 0:256].rearrange("p (j t) -> p j t", t=128)),
                     reads=[('pb', par)], writes=[('rk', 0, b // 4), ('rk', 1, b // 4), ('y2T', b // 4)])
            if h == 0:
                for sl in range(2):
                    wv, wk = load_w(wsrc(rwin_d, 0, 16, 4096 + h * 512 + sl * 256, 256), 16, 256)
                    for b in range(8):
                        v_unit(wv, wk, sl, b, 'dve' if b % 2 else 'act')
            for c in range(8):
                cs = slice(c * 128, (c + 1) * 128)
                pb = nbank()
                S.op('pe', MM(psF[:, pb, 0:128], [(rk[:, j, cs], rq[:, j, cs]) for j in range(2)]),
                     reads=[('rk', j, c // 4) for j in range(2)] + [('rq', j, c // 4) for j in range(2)], writes=[('ps', pb)])
                S.op('dve', TT(sTall[:, c, :], psF[:, pb, 0:128], Mh, ALU.mult), reads=[('ps', pb), 'Mh'], writes=[('sT', c)])
            S.op('act', ACP(Sbh[:, 7, :, :], S32), reads=['S32'], writes=[('Sbh', 7)])
            def bwd_chunk(c, h=h, hc=hc):
                p0 = nbank()
                p1 = nbank()
                S.op('pe', MMS([(psF[:, p0, :], Kb[:, c, 0:128], vv_[:, c, :]), (psF[:, p1, :], Kb[:, c, 128:256], vv_[:, c, :])]),
                     reads=[('Kb', c), ('vv', c, 0), ('vv', c, 1)], writes=[('ps', p0), ('ps', p1)])
                toS = (c % 2 == 0)
                dst = Sout if toS else S32
                for j, pj in enumerate((p0, p1)):
                    S.op('dve', STT(dst[:, j, :], S32[:, j, :], gchb[:, hc], psF[:, pj, :], ALU.mult, ALU.add),
                         reads=['S32', 'gchb', ('ps', pj)], writes=[('tf', j)] if toS else ['S32'])
                if toS:
                    S.dma('sp', DMA(osb_d[c // 2, h].rearrange("(k p) e -> p k e", p=128), Sout), reads=[('tf', 0), ('tf', 1)])
                    if c > 0:
                        S.op('dve', TS(S32, Sout, keep[:, 0:1], None, ALU.mult), reads=[('tf', 0), ('tf', 1), 'keep'], writes=['S32'])
                if c > 0:
                    S.op('act', ACP(Sbh[:, c - 1, :, :], S32), reads=['S32'], writes=[('Sbh', c - 1)])
            for sl in range(2):
                wv, wk = load_w(wsrc(rwin_d, 0, 16, 8192 + h * 512 + sl * 256, 256), 16, 256)
                for b in range(8):
                    bs = slice(b * 128, (b + 1) * 128)
                    pb = nbank()
                    S.op('pe', MM(psF[:, pb, 0:256], [(hT[:, kt, bs], wv[:, kt, :]) for kt in range(KT)]),
                         reads=[wk] + hkeys(b // 4), writes=[('ps', pb)])
                    q = nsq()
                    S.op('act', ACT(sqb[:, q, 0:256], psF[:, pb, 0:256], AF.Silu), reads=[('ps', pb)], writes=[('sq', q)])
                    S.op('dve', TT(gg[:, b, sl * 256:(sl + 1) * 256], sqb[:, q, 0:256], gnrow[:, sl * 256:(sl + 1) * 256], ALU.mult),
                         reads=[('sq', q), 'rstd'], writes=[('gg', b, sl)])
                    if sl == 0:
                        bwd_chunk(7 - b)
            S.dma('sp', DMA(S32, s0f_d[h].rearrange("(k p) e -> p k e", p=128)), writes=['S32'])
            SfbB = Kb[:, 0:4, :].rearrange("p a w -> p (a w)").rearrange("p (x c) -> p x c", c=512)
            Sfbs = [(Sfb, ['SfbA']), (SfbB, ['SfbB'] + [('Kb', b_) for b_ in range(4)])]
            S.op('act', ACP(Sfb, S32), reads=['S32'], writes=['SfbA'])
            pendT = None
            nxt = None
            if h < 7:
                nxt = [load_w(wsrc(rwin_d, 0, 16, 4096 + (h + 1) * 512 + sl * 256, 256), 16, 256) for sl in range(2)]
            for c in range(8):
                cs = slice(c * 128, (c + 1) * 128)
                po = nbank()
                S.op('pe', MM(psF[:, po, :], [(sTall[:, c, :], vv_[:, c, :]),
                                              (qsf[:, 0, cs], Sfbs[c % 2][0][:, 0, :]), (qsf[:, 1, cs], Sfbs[c % 2][0][:, 1, :]),
                                              (qsb[:, 0, cs], Sbh[:, c, 0, :]), (qsb[:, 1, cs], Sbh[:, c, 1, :])]),
                     reads=[('sT', c), ('vv', c, 0), ('vv', c, 1), Sfbs[c % 2][1][0], ('Sbh', c)] + [('qsf', j, c // 4) for j in range(2)] + [('qsb', j, c // 4) for j in range(2)],
                     writes=[('ps', po)])
                p0 = nbank()
                p1 = nbank()
                S.op('pe', MMS([(psF[:, p0, :], Kf[:, c, 0:128], vv_[:, c, :]), (psF[:, p1, :], Kf[:, c, 128:256], vv_[:, c, :])]),
                     reads=[('Kf', c), ('vv', c, 0), ('vv', c, 1)], writes=[('ps', p0), ('ps', p1)])
                toS = (c % 2 == 1)
                dst = Sout if toS else S32
                for j, pj in enumerate((p0, p1)):
                    S.op('dve', STT(dst[:, j, :], S32[:, j, :], gchf[:, hc], psF[:, pj, :], ALU.mult, ALU.add),
                         reads=['S32', 'gchf', ('ps', pj)], writes=[('tf', j)] if toS else ['S32'])
                if toS:
                    S.dma('sp', DMA(osf_d[c // 2, h].rearrange("(k p) e -> p k e", p=128), Sout), reads=[('tf', 0), ('tf', 1)])
                    if c < 7:
                        S.op('dve', TS(S32, Sout, keep[:, 0:1], None, ALU.mult), reads=[('tf', 0), ('tf', 1), 'keep'], writes=['S32'])
                if c < 7:
                    S.op('act', ACP(Sfbs[(c + 1) % 2][0], S32), reads=['S32'], writes=Sfbs[(c + 1) % 2][1])
                S.op('dve', lambda e, po=po: e.bn_stats(out=stt[:, 0, :], in_=psF[:, po, :]), reads=[('ps', po)], writes=['stt0'])
                S.op('dve', lambda e: e.bn_aggr(out=small[:, 56:58], in_=stt[:, 0, :]), reads=['stt0'], writes=['mv'])
                S.op('act', ACT(small[:, 58:59], small[:, 57:58], AF.Sqrt, bias=epsc[:, 0:1]), reads=['mv', 'epsc'], writes=['sd'])
                S.op('dve', STT(tmpf[:, 2, :], psF[:, po, :], small[:, 56:57], gg[:, c, :], ALU.subtract, ALU.mult),
                     reads=[('ps', po), 'mv', ('gg', c, 0), ('gg', c, 1)], writes=[('tf', 2)])
                S.op('dve', lambda e: e.reciprocal(out=small[:, 59:60], in_=small[:, 58:59]), reads=['sd'], writes=['rs'])
                q = nsq()
                S.op('act', ACT(sqb[:, q, :], tmpf[:, 2, :], AF.Identity, scale=small[:, 59:60]), reads=[('tf', 2), 'rs'], writes=[('sq', q)])
                def emitT(c=c, q=q, cs=cs):
                    par = c % 2
                    S.op('pe', TRS([(psB[:, par, j * 128:(j + 1) * 128], sqb[:, q, j * 128:(j + 1) * 128]) for j in range(4)], identb[:]),
                         reads=[('sq', q), 'identb'], writes=[('pb', par)])
                    S.op('act', ACP(y2T[:, 0:4, cs], psB[:, par, 0:512].rearrange("p (k t) -> p k t", t=128)),
                         reads=[('pb', par)] + [('sT', c_) for c_ in range(8)], writes=[('y2T', c // 4)])
                if pendT is not None:
                    pendT()
                pendT = emitT
                if nxt is not None and c >= 1:
                    for sl in range(2):
                        v_unit(nxt[sl][0], nxt[sl][1], sl, c - 1, 'act')
            pendT()
            if nxt is not None:
                for sl in range(2):
                    v_unit(nxt[sl][0], nxt[sl][1], sl, 7, 'act')
            slabs = [(wsrc(rwout_d, h * 512, 4, sl * 1024, 1024), 4, 1024, sl * 8) for sl in range(2)]
            resid_gemm(slabs, lambda k, gs: y2T[:, k, gs], lambda g: [('y2T', g)], modT[:, 1, 32:48], ('mod', 1))
        S.barrier()
        phase_end('l1')

        ffn(1)
        S.barrier()
        phase_end('ffn1')

        yst = carve(0, [2, D], F32)
        fT = carve(16384, [KT, 512], F32)
        for g in range(2):
            gs = slice(g * 512, (g + 1) * 512)
            pb = nbank()
            for kt in range(KT):
                q = nsq()
                S.op('act', ACT(sqb[:, q, :], xT[:, kt, gs], AF.Square), reads=[('xT', kt, g)], writes=[('sq', q)])
                S.op('pe', MM1(psF[:, pb, :], onesb[:], sqb[:, q, :], kt == 0, kt == KT - 1),
                     reads=[('sq', q), 'onesb'], writes=[('ps', pb)])
            S.op('act', ACT(rstd[:], psF[:, pb, :], AF.Sqrt, scale=1.0 / D, bias=epsc[:, 0:1]), reads=[('ps', pb), 'epsc'], writes=['rstd'])
            S.op('dve', lambda e: e.reciprocal(out=rstd[:], in_=rstd[:]), reads=['rstd'], writes=['rstd'])
            for kt in range(KT):
                S.op('dve', STT(fT[:, kt, :], xT[:, kt, gs], fing[:, kt:kt + 1], rstd[:], ALU.mult, ALU.mult),
                     reads=[('xT', kt, g), 'rstd', 'fing'], writes=[('fT', kt)])
            for bb in range(4):
                b = g * 4 + bb
                for k4 in range(4):
                    pb2 = nbank()
                    S.op('pe', TRS([(psF[:, pb2, j * 128:(j + 1) * 128], fT[:, k4 * 4 + j, bb * 128:(bb + 1) * 128]) for j in range(4)], identf[:]),
                         reads=[('fT', k4 * 4 + j) for j in range(4)] + ['identf'], writes=[('ps', pb2)])
                    if k4 % 2:
                        S.op('dve', CP(yst[:, b % 2, k4 * 512:(k4 + 1) * 512], psF[:, pb2, :]), reads=[('ps', pb2)], writes=[('yst', b % 2, k4)])
                    else:
                        S.op('act', ACP(yst[:, b % 2, k4 * 512:(k4 + 1) * 512], psF[:, pb2, :]), reads=[('ps', pb2)], writes=[('yst', b % 2, k4)])
                S.dma('sp', DMA(y_d[b * 128:(b + 1) * 128, :], yst[:, b % 2, :]), reads=[('yst', b % 2, k4) for k4 in range(4)])
    except _Stop:
        pass
    S.finish()
    with nc.Block() as block:
        S.replay(block)
    es.close()
    return nc


_NC_CACHE = {}


def _consts():
    ident = np.eye(128, dtype=np.float32)
    d = np.arange(128) % 64
    a = d // 32
    b = (d // 16) % 2
    f = d % 16
    swap = np.arange(128) + np.where(b == 0, 16, -16)
    perm = np.zeros((128, 128), np.float32)
    perm[swap, np.arange(128)] = 1.0
    t = np.arange(T)
    t_row = (t // 64).astype(np.float32)
    t_col = (t % 64).astype(np.float32)
    inv = (np.float32(10000.0) ** (-(np.arange(16, dtype=np.float32)) / np.float32(16))).astype(np.float32)
    ang = np.where(a[:, None] == 0, t_row[None, :], t_col[None, :]).astype(np.float32) * inv[f][:, None]
    ang = ang.astype(np.float32)
    C_s = np.cos(ang).astype(np.float32)
    S_s = (np.where(b[:, None] == 0, -1.0, 1.0) * np.sin(ang)).astype(np.float32)
    C_p = np.ones((128, T), np.float32)
    S_p = np.zeros((128, T), np.float32)
    qq = np.arange(128)[:, None]
    kk = np.arange(128)[None, :]
    m_s = np.zeros((128, 2, 896), np.float32)
    for par in range(2):
        m_s[:, par, 0:128] = np.where(kk >= qq, 0.0, NEG)
        m_s[:, par, 256:384] = np.where(kk <= qq, 0.0, NEG)
    m_p = np.zeros((128, 2, 896), np.float32)
    m_p[:, :, 384:896] = NEG
    m_p[:, 0, 0:128] = NEG
    m_p[:, 1, 256:384] = NEG
    jj = np.arange(128)[:, None].astype(np.float32)
    ii = np.arange(128)[None, :].astype(np.float32)
    retc = np.zeros((128, 4, 128), np.float32)
    retc[:, 0] = np.maximum(ii - jj, 0)
    retc[:, 1] = np.maximum(jj - ii, 0)
    retc[:, 2] = (ii >= jj) / 16.0
    retc[:, 3] = (jj >= ii) / 16.0
    iot = np.zeros((128, 2, 512), np.float32)
    iot[:, 0, :] = np.tile(np.arange(128, dtype=np.float32) + 1, 4)[None, :]
    iot[:, 1, :] = np.tile(128 - np.arange(128, dtype=np.float32), 4)[None, :]
    cols = np.zeros((128, 4), np.float32)
    cols[:, 0] = 127 - np.arange(128)
    cols[:, 1] = np.arange(128)
    cols[:, 2] = 128.0
    cols[:, 3] = 1.0
    return dict(ident=ident, perm=perm, C_s=C_s, S_s=S_s, C_p=C_p, S_p=S_p, m_s=m_s, m_p=m_p, retc=retc, iot=iot, cols=cols)


def _colT(v):
    v = np.asarray(v, np.float32)
    return np.ascontiguousarray(np.moveaxis(v.reshape(v.shape[:-1] + (-1, 128)), -1, 0))


def kernel(x_prompt, x_sample, cache_attn_k, cache_attn_v, state_ret_fwd, state_ret_bwd,
           c, c_ctx, ada_w, ada_b, norm_mix_g, norm_ffn_g, even_w_in, even_w_out,
           attn_sink, sgu_ln_g, sgu_ln_b, sgu_w, sgu_b, ret_w_in, ret_w_out, ret_gn_g,
           ret_decay_fwd, ret_decay_bwd, ffn_w_gate, ffn_w_up, ffn_w_down, final_g):
    f32 = lambda a: np.ascontiguousarray(np.asarray(a, dtype=np.float32))
    if 'nc' not in _NC_CACHE:
        _NC_CACHE['nc'] = build_program()
    nc = _NC_CACHE['nc']
    K = _consts()
    rep = lambda v: np.ascontiguousarray(np.broadcast_to(f32(v).reshape(1, -1), (128, f32(v).size)))
    shared = {
        "ident": K['ident'], "perm": K['perm'], "retc": K['retc'], "iotarep": K['iot'], "colsc": K['cols'],
        "ada_w": f32(ada_w),
        "ada_bT": _colT(ada_b),
        "nmgT": _colT(norm_mix_g), "nfgT": _colT(norm_ffn_g), "fingT": _colT(final_g),
        "even_w_in": f32(even_w_in[0]), "even_w_out": f32(even_w_out[0]),
        "sinkR": rep(attn_sink[0]), "lngR": rep(sgu_ln_g[0]), "lnbR": rep(sgu_ln_b[0]),
        "sguwT": np.ascontiguousarray(np.transpose(f32(sgu_w[0]), (2, 0, 1))),
        "sgubT": np.ascontiguousarray(f32(sgu_b[0]).T),
        "ret_w_in": f32(ret_w_in[0]), "ret_w_out": f32(ret_w_out[0]),
        "gngR": rep(ret_gn_g[0]), "decfR": rep(ret_decay_fwd[0]), "decbR": rep(ret_decay_bwd[0]),
        "ffn_w_gate": f32(ffn_w_gate), "ffn_w_up": f32(ffn_w_up), "ffn_w_down": f32(ffn_w_down),
    }
    xp = f32(x_prompt)
    xs = f32(x_sample)
    zkv = np.zeros((512, 256), np.float32)
    zst = np.zeros((8, 256, 512), np.float32)
    in_maps = []
    for core in range(8):
        m = dict(shared)
        if core < 4:
            m["x"] = np.ascontiguousarray(xp[core * 4:(core + 1) * 4].reshape(T, D))
            m["cvec"] = _colT(c_ctx)
            m["kctx"] = zkv
            m["vctx"] = zkv
            m["s0f"] = zst
            m["s0b"] = zst
            m["keep"] = np.zeros((128, 1), np.float32)
            m["amask"] = K['m_p']
            m["ropeC"] = K['C_p']
            m["ropeS"] = K['S_p']
        else:
            bi = core - 4
            m["x"] = np.ascontiguousarray(xs[bi])
            m["cvec"] = _colT(f32(c)[bi])
            m["kctx"] = np.ascontiguousarray(f32(cache_attn_k)[bi, 0].reshape(512, 256))
            m["vctx"] = np.ascontiguousarray(f32(cache_attn_v)[bi, 0].reshape(512, 256))
            m["s0f"] = np.ascontiguousarray(f32(state_ret_fwd)[bi, 0])
            m["s0b"] = np.ascontiguousarray(f32(state_ret_bwd)[bi, 0])
            m["keep"] = np.ones((128, 1), np.float32)
            m["amask"] = K['m_s']
            m["ropeC"] = K['C_s']
            m["ropeS"] = K['S_s']
        in_maps.append(m)
    res = run_bass_kernel_spmd(nc, in_maps, core_ids=list(range(8)))
    R = res.results
    y_prompt = np.concatenate([R[cidx]["y"].reshape(4, 256, D) for cidx in range(4)], axis=0).astype(np.float32)
    y_sample = np.stack([R[4 + b_]["y"] for b_ in range(4)], axis=0).astype(np.float32)
    okv = np.concatenate([R[cidx]["okv"].reshape(4, 256, 512) for cidx in range(4)], axis=0)
    new_k = np.ascontiguousarray(okv[:, :, 0:256]).reshape(16, 1, 256, 4, 64).astype(np.float32)
    new_v = np.ascontiguousarray(okv[:, :, 256:512]).reshape(16, 1, 256, 4, 64).astype(np.float32)
    new_sf = np.concatenate([R[cidx]["osf"] for cidx in range(4)], axis=0).reshape(16, 1, 8, 256, 512).astype(np.float32)
    new_sb = np.concatenate([R[cidx]["osb"] for cidx in range(4)], axis=0).reshape(16, 1, 8, 256, 512).astype(np.float32)
    return (y_prompt, y_sample, new_k, new_v, new_sf, new_sb)
```
